# Optimizing a Trainium2 kernel written in Bass

```python
import jax
import jax.numpy as jnp
from jax import lax
import numpy as np

D_MODEL = 1024
BATCH = 16
SEQ = 4096
DEPTH = 4

CTX_LEN = 256
GRID_W = 64
HEAD_DIM = 64
D_MIX = D_MODEL
D_LRU = D_MIX // 2
LRU_BLOCKS = D_LRU // HEAD_DIM
LRU_BLOCK = D_LRU // LRU_BLOCKS
LRU_CONV_W = 4
LRU_PAD = (2, 1)
LRU_C = 8.0
GA_HEADS = D_MIX // 4 // HEAD_DIM
GA_KV = GA_HEADS // 2
WA_HEADS = D_MIX // 4 // HEAD_DIM
WA_KV = WA_HEADS // 2
WINDOW = 128
Q_BLOCK = 128
D_FF = 2816
FFN_CONV_W = 3
FFN_PAD = (1, 1)
ROPE_BASE = 10000.0
EPS = 1e-6
NEG_INF = -1e30
ATTN_SCALE = HEAD_DIM ** -0.5
IN_SPLITS = (D_LRU, D_LRU, GA_HEADS * HEAD_DIM, GA_KV * HEAD_DIM, GA_KV * HEAD_DIM,
             WA_HEADS * HEAD_DIM, WA_KV * HEAD_DIM, WA_KV * HEAD_DIM)
D_IN = sum(IN_SPLITS)

kernel_name = "hymba_style_rglru_gqa_swa_convffn_dit"


def rms_norm(t, g):
    tf = t.astype(jnp.float32)
    y = tf * lax.rsqrt(jnp.mean(tf * tf, axis=-1, keepdims=True) + EPS)
    return (y * g.astype(jnp.float32)).astype(t.dtype)


def modulate(t, shift, scale):
    return t * (1.0 + scale) + shift


def dwconv(t, w, b, pad_l, pad_r):
    ch = t.shape[-1]
    y = lax.conv_general_dilated(t, w[:, None, :], window_strides=(1,), padding=[(pad_l, pad_r)],
                                 dimension_numbers=("NWC", "WIO", "NWC"), feature_group_count=ch)
    return y + b


def split_heads(t, n_heads):
    return t.reshape(t.shape[0], t.shape[1], n_heads, HEAD_DIM)


def axial_rope_tables(n):
    rows = n // GRID_W
    row = jnp.repeat(jnp.arange(rows), GRID_W).astype(jnp.float32)
    col = jnp.tile(jnp.arange(GRID_W), rows).astype(jnp.float32)
    n_freq = HEAD_DIM // 4
    inv = ROPE_BASE ** (-jnp.arange(n_freq, dtype=jnp.float32) / n_freq)
    ang = jnp.concatenate([row[:, None] * inv, col[:, None] * inv], axis=-1)
    return jnp.cos(ang), jnp.sin(ang)


def apply_rope(t, cos, sin):
    tf = t.astype(jnp.float32)
    half = HEAD_DIM // 2
    t1, t2 = tf[..., :half], tf[..., half:]
    cs, sn = cos[None, :, None, :], sin[None, :, None, :]
    return jnp.concatenate([t1 * cs - t2 * sn, t1 * sn + t2 * cs], axis=-1).astype(t.dtype)


def _lru_combine(left, right):
    a_l, b_l = left
    a_r, b_r = right
    return a_l * a_r, a_r * b_l + b_r


def rglru_scan(u, w_a, b_a, w_x, b_x, lam, h0, reverse):
    bsz, n, _ = u.shape
    ub = u.reshape(bsz, n, LRU_BLOCKS, LRU_BLOCK)
    r = jax.nn.sigmoid((jnp.einsum("btnc,ncd->btnd", ub, w_a).reshape(bsz, n, D_LRU) + b_a).astype(jnp.float32))
    i = jax.nn.sigmoid((jnp.einsum("btnc,ncd->btnd", ub, w_x).reshape(bsz, n, D_LRU) + b_x).astype(jnp.float32))
    log_a = -LRU_C * r * jax.nn.softplus(-lam.astype(jnp.float32))
    a = jnp.exp(log_a)
    drive = jnp.sqrt(-jnp.expm1(2.0 * log_a)) * i * u.astype(jnp.float32)
    if h0 is not None:
        edge = n - 1 if reverse else 0
        drive = drive.at[:, edge].add(a[:, edge] * h0)
    _, h = lax.associative_scan(_lru_combine, (a, drive), reverse=reverse, axis=1)
    return h


def global_attention(q, k, v):
    bsz, n, n_h, d = q.shape
    g = k.shape[2]
    nb = n // Q_BLOCK
    qb = q.reshape(bsz, nb, Q_BLOCK, g, n_h // g, d).swapaxes(0, 1)

    def one(qn):
        s = jnp.einsum("bqgrd,bkgd->bgrqk", qn, k).astype(jnp.float32)
        p = jax.nn.softmax(s, axis=-1).astype(v.dtype)
        return jnp.einsum("bgrqk,bkgd->bqgrd", p, v)

    o = lax.map(one, qb)
    return o.swapaxes(0, 1).reshape(bsz, n, n_h * d)


def window_attention(q, k, v, k_ctx, v_ctx, sink):
    bsz, n, n_h, d = q.shape
    g = k.shape[2]
    r = n_h // g
    nb = n // Q_BLOCK
    pad = ((0, 0), (Q_BLOCK, Q_BLOCK), (0, 0), (0, 0))

    def band(t):
        tb = jnp.pad(t, pad).reshape(bsz, nb + 2, Q_BLOCK, g, d)
        return jnp.concatenate([tb[:, :-2], tb[:, 1:-1], tb[:, 2:]], axis=2)

    kb, vb = band(k), band(v)
    qb = q.reshape(bsz, nb, Q_BLOCK, g, r, d)
    blk = jnp.arange(nb)
    qpos = blk[:, None] * Q_BLOCK + jnp.arange(Q_BLOCK)[None, :]
    kpos = (blk[:, None] - 1) * Q_BLOCK + jnp.arange(3 * Q_BLOCK)[None, :]
    kp = kpos[:, None, :]
    mask = (jnp.abs(kp - qpos[:, :, None]) <= WINDOW) & (kp >= 0) & (kp < n)
    sink_gr = sink.astype(jnp.float32).reshape(g, r)
    n_loc = 3 * Q_BLOCK
    n_ctx = k_ctx.shape[1]

    def one(args):
        qn, kn, vn, mn = args
        s_loc = jnp.where(mn, jnp.einsum("bqgrd,bkgd->bgrqk", qn, kn).astype(jnp.float32), NEG_INF)
        s_ctx = jnp.einsum("bqgrd,blgd->bgrql", qn, k_ctx).astype(jnp.float32)
        s_sink = jnp.broadcast_to(sink_gr[None, :, :, None, None], s_ctx.shape[:-1] + (1,))
        p = jax.nn.softmax(jnp.concatenate([s_loc, s_ctx, s_sink], axis=-1), axis=-1).astype(vn.dtype)
        return (jnp.einsum("bgrqk,bkgd->bqgrd", p[..., :n_loc], vn)
                + jnp.einsum("bgrql,blgd->bqgrd", p[..., n_loc:n_loc + n_ctx], v_ctx))

    o = lax.map(one, (qb.swapaxes(0, 1), kb.swapaxes(0, 1), vb.swapaxes(0, 1), mask))
    return o.swapaxes(0, 1).reshape(bsz, n, n_h * d)


def sink_attention(q, k, v, sink):
    bsz, n, n_h, d = q.shape
    g = k.shape[2]
    qg = q.reshape(bsz, n, g, n_h // g, d)
    s = jnp.einsum("bqgrd,bkgd->bgrqk", qg, k).astype(jnp.float32)
    s_sink = jnp.broadcast_to(sink.astype(jnp.float32).reshape(g, n_h // g)[None, :, :, None, None], s.shape[:-1] + (1,))
    p = jax.nn.softmax(jnp.concatenate([s, s_sink], axis=-1), axis=-1)[..., :-1].astype(v.dtype)
    return jnp.einsum("bgrqk,bkgd->bqgrd", p, v).reshape(bsz, n, n_h * d)


def hybrid_mixer(a_lat, a_ctx, cos, sin, w_in, conv_w, conv_b, w_a, b_a, w_x, b_x, lam, q_g, k_g, sink, need_ctx):
    offsets = np.cumsum(IN_SPLITS)[:-1].tolist()
    r_l, gate_l, gq_l, gk_l, gv_l, wq_l, wk_l, wv_l = jnp.split(a_lat @ w_in, offsets, axis=-1)
    r_c, gate_c, gq_c, gk_c, gv_c, wq_c, wk_c, wv_c = jnp.split(a_ctx @ w_in, offsets, axis=-1)

    u_l = dwconv(r_l, conv_w, conv_b, LRU_PAD[0], LRU_PAD[1])
    u_c = dwconv(r_c, conv_w, conv_b, LRU_PAD[0], LRU_PAD[1])
    hs_l, hs_c = [], []
    for d_idx, reverse in enumerate((False, True)):
        h_c_d = rglru_scan(u_c, w_a[d_idx], b_a[d_idx], w_x[d_idx], b_x[d_idx], lam[d_idx], None, reverse)
        h0 = h_c_d[:, 0] if reverse else h_c_d[:, -1]
        hs_l.append(rglru_scan(u_l, w_a[d_idx], b_a[d_idx], w_x[d_idx], b_x[d_idx], lam[d_idx], h0, reverse))
        hs_c.append(h_c_d)
    y_a_l = ((hs_l[0] + hs_l[1]) * jax.nn.gelu(gate_l.astype(jnp.float32))).astype(a_lat.dtype)

    kg_c = rms_norm(split_heads(gk_c, GA_KV), k_g)
    vg_c = split_heads(gv_c, GA_KV)
    q_b = apply_rope(rms_norm(split_heads(gq_l, GA_HEADS), q_g), cos, sin) * ATTN_SCALE
    k_b = jnp.concatenate([apply_rope(rms_norm(split_heads(gk_l, GA_KV), k_g), cos, sin), kg_c], axis=1)
    v_b = jnp.concatenate([split_heads(gv_l, GA_KV), vg_c], axis=1)
    y_b_l = global_attention(q_b, k_b, v_b)

    kw_c = split_heads(wk_c, WA_KV)
    vw_c = split_heads(wv_c, WA_KV)
    y_c_l = window_attention(apply_rope(split_heads(wq_l, WA_HEADS), cos, sin) * ATTN_SCALE,
                             apply_rope(split_heads(wk_l, WA_KV), cos, sin), split_heads(wv_l, WA_KV),
                             kw_c, vw_c, sink)

    mix_lat = jnp.concatenate([y_a_l, y_b_l, y_c_l], axis=-1)
    if not need_ctx:
        return mix_lat, None
    y_a_c = ((hs_c[0] + hs_c[1]) * jax.nn.gelu(gate_c.astype(jnp.float32))).astype(a_ctx.dtype)
    y_b_c = global_attention(rms_norm(split_heads(gq_c, GA_HEADS), q_g) * ATTN_SCALE, kg_c, vg_c)
    y_c_c = sink_attention(split_heads(wq_c, WA_HEADS) * ATTN_SCALE, kw_c, vw_c, sink)
    return mix_lat, jnp.concatenate([y_a_c, y_b_c, y_c_c], axis=-1)


def conv_ffn(t, w_up, conv_w, conv_b, w_down):
    u = dwconv(t @ w_up, conv_w, conv_b, FFN_PAD[0], FFN_PAD[1])
    g, v = jnp.split(u, 2, axis=-1)
    return (jax.nn.silu(g) * v) @ w_down


def setup_inputs(seed: int = 0) -> dict:
    key = jax.random.key(seed)
    ks = jax.random.split(key, 26)
    f32 = jnp.float32

    def nrm(k, shape, s):
        return jax.random.normal(k, shape, f32) * s

    u = jax.random.uniform(ks[12], (DEPTH, 2, D_LRU), f32, 0.9, 0.999)
    a0 = u ** (1.0 / LRU_C)
    lam = jnp.log(a0) - jnp.log1p(-a0)
    return {
        "x": nrm(ks[0], (BATCH, SEQ, D_MODEL), 1.0),
        "c": nrm(ks[1], (BATCH, D_MODEL), 1.0),
        "ctx": nrm(ks[2], (BATCH, CTX_LEN, D_MODEL), 1.0),
        "c_ctx": nrm(ks[3], (D_MODEL,), 1.0),
        "w_mod": nrm(ks[4], (DEPTH, D_MODEL, 6 * D_MODEL), 0.5 * D_MODEL ** -0.5),
        "b_mod": nrm(ks[5], (DEPTH, 6 * D_MODEL), 0.02),
        "norm1_g": 1.0 + nrm(ks[6], (DEPTH, D_MODEL), 0.05),
        "w_in": nrm(ks[7], (DEPTH, D_MODEL, D_IN), D_MODEL ** -0.5),
        "lru_conv_w": nrm(ks[8], (DEPTH, LRU_CONV_W, D_LRU), LRU_CONV_W ** -0.5),
        "lru_conv_b": nrm(ks[9], (DEPTH, D_LRU), 0.01),
        "lru_w_a": nrm(ks[10], (DEPTH, 2, LRU_BLOCKS, LRU_BLOCK, LRU_BLOCK), LRU_BLOCK ** -0.5),
        "lru_b_a": nrm(ks[11], (DEPTH, 2, D_LRU), 0.01),
        "lru_w_x": nrm(ks[13], (DEPTH, 2, LRU_BLOCKS, LRU_BLOCK, LRU_BLOCK), LRU_BLOCK ** -0.5),
        "lru_b_x": nrm(ks[14], (DEPTH, 2, D_LRU), 0.01),
        "lru_lam": lam,
        "ga_q_norm_g": 1.0 + nrm(ks[15], (DEPTH, HEAD_DIM), 0.05),
        "ga_k_norm_g": 1.0 + nrm(ks[16], (DEPTH, HEAD_DIM), 0.05),
        "wa_sink": nrm(ks[17], (DEPTH, WA_HEADS), 0.5),
        "w_out": nrm(ks[18], (DEPTH, D_MIX, D_MODEL), D_MIX ** -0.5),
        "norm2_g": 1.0 + nrm(ks[19], (DEPTH, D_MODEL), 0.05),
        "w_up": nrm(ks[20], (DEPTH, D_MODEL, 2 * D_FF), D_MODEL ** -0.5),
        "ffn_conv_w": nrm(ks[21], (DEPTH, FFN_CONV_W, 2 * D_FF), FFN_CONV_W ** -0.5),
        "ffn_conv_b": nrm(ks[22], (DEPTH, 2 * D_FF), 0.01),
        "w_down": nrm(ks[23], (DEPTH, D_FF, D_MODEL), D_FF ** -0.5),
        "final_norm_g": 1.0 + nrm(ks[24], (D_MODEL,), 0.05),
    }


def reference(x, c, ctx, c_ctx, w_mod, b_mod, norm1_g, w_in, lru_conv_w, lru_conv_b, lru_w_a, lru_b_a,
              lru_w_x, lru_b_x, lru_lam, ga_q_norm_g, ga_k_norm_g, wa_sink, w_out, norm2_g, w_up,
              ffn_conv_w, ffn_conv_b, w_down, final_norm_g):
    n = x.shape[1]
    cos, sin = axial_rope_tables(n)
    cond_lat = jax.nn.silu(c)[:, None, :]
    cond_ctx = jax.nn.silu(c_ctx)[None, None, :]
    h, hc = x, ctx
    for l in range(DEPTH):
        need_ctx = l < DEPTH - 1
        sh1, sc1, g1, sh2, sc2, g2 = jnp.split(cond_lat @ w_mod[l] + b_mod[l], 6, axis=-1)
        csh1, csc1, cg1, csh2, csc2, cg2 = jnp.split(cond_ctx @ w_mod[l] + b_mod[l], 6, axis=-1)
        a_lat = modulate(rms_norm(h, norm1_g[l]), sh1, sc1)
        a_ctx = modulate(rms_norm(hc, norm1_g[l]), csh1, csc1)
        mix_lat, mix_ctx = hybrid_mixer(a_lat, a_ctx, cos, sin, w_in[l], lru_conv_w[l], lru_conv_b[l],
                                        lru_w_a[l], lru_b_a[l], lru_w_x[l], lru_b_x[l], lru_lam[l],
                                        ga_q_norm_g[l], ga_k_norm_g[l], wa_sink[l], need_ctx)
        h = h + g1 * (mix_lat @ w_out[l])
        h = h + g2 * conv_ffn(modulate(rms_norm(h, norm2_g[l]), sh2, sc2),
                              w_up[l], ffn_conv_w[l], ffn_conv_b[l], w_down[l])
        if need_ctx:
            hc = hc + cg1 * (mix_ctx @ w_out[l])
            hc = hc + cg2 * conv_ffn(modulate(rms_norm(hc, norm2_g[l]), csh2, csc2),
                                     w_up[l], ffn_conv_w[l], ffn_conv_b[l], w_down[l])
    return rms_norm(h, final_norm_g)
```

```python
import numpy as np
import ml_dtypes
import concourse.bass as bass
import concourse.mybir as mybir
from concourse.bass_utils import run_bass_kernel_spmd

F32 = mybir.dt.float32
BF16 = mybir.dt.bfloat16
ALU = mybir.AluOpType
AF = mybir.ActivationFunctionType

D = 1024
CTX = 256
DFF = 2816
NPAIR = 22
EPS = 1e-6
ENGS = ("pe", "act", "dve", "pool", "sp")
DEBUG = False
B_SCALE = 1.0
ATT_DELAY = 60.0
CVT_COST = 50.0


class Sched:
    def __init__(self, nc, n_dma_sems=(("sp", 24), ("pool", 12), ("act", 4))):
        self.nc = nc
        self.q = {e: [] for e in ENGS}
        self.cnt = {e: 0 for e in ENGS}
        self.seen = {e: {} for e in ENGS}
        self.esem = {}
        self.dsem = {}
        self.dsem_rr = {}
        self.dsem_val = {}
        self._ctx = []
        for e in ENGS:
            cm = nc.semaphore("s_" + e)
            self.esem[e] = cm.__enter__()
            self._ctx.append(cm)
        for e, n in n_dma_sems:
            lst = []
            for i in range(n):
                cm = nc.semaphore("d_%s_%d" % (e, i))
                s = cm.__enter__()
                self._ctx.append(cm)
                lst.append(s)
                self.dsem_val[id(s)] = 0
            self.dsem[e] = lst
            self.dsem_rr[e] = 0
        self.lastw = {}
        self.readers = {}
        self.n_wait = 0
        self.n_ins = 0

    def close(self):
        for cm in reversed(self._ctx):
            cm.__exit__(None, None, None)

    def _deps(self, eng, reads, writes):
        waits = {}

        def add(tok):
            sem, val, src = tok
            if src == "pe" and eng == "pe":
                return
            key = id(sem)
            if self.seen[eng].get(key, 0) >= val:
                return
            if key not in waits or waits[key][1] < val:
                waits[key] = (sem, val)

        for k in reads:
            t = self.lastw.get(k)
            if t is not None:
                add(t)
        for k in writes:
            t = self.lastw.get(k)
            if t is not None:
                add(t)
            for t in self.readers.get(k, {}).values():
                add(t)
        for key, (sem, val) in waits.items():
            self.seen[eng][key] = val
        return list(waits.values())

    def _commit(self, tok, reads, writes):
        for k in reads:
            r = self.readers.setdefault(k, {})
            r[(id(tok[0]), tok[2] == "pe")] = tok
        for k in writes:
            self.lastw[k] = tok
            self.readers[k] = {}

    def op(self, eng, fn, reads=(), writes=()):
        waits = self._deps(eng, reads, writes)
        self.cnt[eng] += 1
        tok = (self.esem[eng], self.cnt[eng], eng)
        self.q[eng].append((fn, waits, (self.esem[eng], 1)))
        self._commit(tok, reads, writes)
        self.n_ins += 1
        self.n_wait += len(waits)
        return tok

    def dma(self, eng, fn, reads=(), writes=()):
        lst = self.dsem[eng]
        s = lst[self.dsem_rr[eng] % len(lst)]
        self.dsem_rr[eng] += 1
        waits = self._deps(eng, reads, writes)
        prev = self.dsem_val[id(s)]
        if prev > 0 and self.seen[eng].get(id(s), 0) < prev:
            waits = [w for w in waits if w[0] is not s]
            waits.append((s, prev))
            self.seen[eng][id(s)] = prev
        self.dsem_val[id(s)] = prev + 16
        tok = (s, prev + 16, "dma")
        self.q[eng].append((fn, waits, (s, 16)))
        self._commit(tok, reads, writes)
        self.n_ins += 1
        self.n_wait += len(waits)
        return tok

    def handoff(self, old_keys, new_keys):
        toks = {}
        for k in old_keys:
            cands = []
            t = self.lastw.get(k)
            if t is not None:
                cands.append(t)
            cands.extend(self.readers.get(k, {}).values())
            for t in cands:
                kk = (id(t[0]), t[2] == "pe")
                if kk not in toks or toks[kk][1] < t[1]:
                    toks[kk] = t
        for k in new_keys:
            self.lastw[k] = None
            self.readers[k] = dict(toks)

    def wait_all(self, eng):
        waits = []
        for e in ENGS:
            if self.cnt[e] > 0 and e != eng:
                waits.append((self.esem[e], self.cnt[e]))
        for e, lst in self.dsem.items():
            for s in lst:
                v = self.dsem_val[id(s)]
                if v > 0:
                    waits.append((s, v))
        self.q[eng].append((None, waits, None))

    def emit(self):
        nc = self.nc
        q = self.q

        def run(engobj, items):
            for fn, waits, inc in items:
                if fn is None:
                    for sem, val in waits:
                        engobj.wait_ge(sem, val)
                    continue
                for sem, val in waits[:-1]:
                    engobj.wait_ge(sem, val)
                ins = fn(engobj)
                if waits:
                    sem, val = waits[-1]
                    ins._wait_ge(sem, val)
                ins.then_inc(inc[0], inc[1])

        with nc.Block() as block:
            @block.sync
            def _(e):
                run(e, q["sp"])

            @block.tensor
            def _(e):
                run(e, q["pe"])

            @block.scalar
            def _(e):
                run(e, q["act"])

            @block.vector
            def _(e):
                run(e, q["dve"])

            @block.gpsimd
            def _(e):
                run(e, q["pool"])


def small_layout(L):
    off = {}
    o = 0
    for name, n in (("bmod", L * 48), ("n1g", L * 8), ("n2g", L * 8), ("fng", 8), ("cw", L * 16),
                    ("cb", L * 4), ("bax", L * 16), ("lam", L * 8), ("qg", L * 2), ("sink", L * 4),
                    ("fw", L * 44 * 3), ("fb", L * 44), ("cvec", 32)):
        off[name] = o
        o += n
    return off, o


W_IN_PERM = None


def w_in_perm():
    r = list(range(0, 512))
    gate = list(range(512, 1024))
    gq = lambda h: list(range(1024 + 64 * h, 1024 + 64 * h + 64))
    gk = list(range(1280, 1408))
    gv = list(range(1408, 1536))
    wq = lambda h: list(range(1536 + 64 * h, 1536 + 64 * h + 64))
    wk = list(range(1792, 1920))
    wv = list(range(1920, 2048))
    return np.array(r + gate + gq(0) + gq(2) + gq(1) + gq(3) + gk + wq(0) + wq(2) + wq(1) + wq(3) + wk + gv + wv)


def build(S_len, NSEQ, L, wplan=None):
    T = CTX + S_len
    NKB = T // 128
    NB = S_len // 128
    WB = S_len + 262
    WU = S_len + 259
    nc = bass.Bass("TRN2", target_bir_lowering=False)
    S = Sched(nc)
    SO, NS = small_layout(L)

    def din(name, shape, dt=F32):
        return nc.dram_tensor(name, list(shape), dt, kind="ExternalInput").ap()

    def dscr(name, shape, dt):
        return nc.dram_tensor(name, list(shape), dt, kind=("ExternalOutput" if DEBUG else "Internal")).ap()

    x_d = din("x", [NSEQ, S_len, D])
    ctx_d = din("ctx", [NSEQ, CTX, D])
    out_d = nc.dram_tensor("out", [NSEQ, S_len, D], F32, kind="ExternalOutput").ap()
    small_d = din("small", [128, NS])
    cf32_d = din("cf32", [128, 256])
    cb16_d = din("cb16", [128, 768], BF16)
    rope_d = din("rope", [2, 128, S_len])
    wmod_d = din("w_mod", [L, D, 6 * D])
    win_d = [din("w_in%d" % l, [1024, 2048]) for l in range(L)]
    wout_d = [din("w_out%d" % l, [512, 2048]) for l in range(L)]
    wup_d = [din("w_up%d" % l, [NPAIR * 128, 2048]) for l in range(L)]
    wdn_d = [din("w_dn%d" % l, [8 * 128, NPAIR * 128]) for l in range(L)]
    wbd_d = [din("w_bd%d" % l, [128, 16 * 128]) for l in range(L)]
    win_b = [dscr("w_in_b%d" % l, [1024, 2048], BF16) for l in range(L)]
    wout_b = [dscr("w_out_b%d" % l, [512, 2048], BF16) for l in range(L)]
    wup_b = [dscr("w_up_b%d" % l, [NPAIR * 128, 2048], BF16) for l in range(L)]
    wdn_b = [dscr("w_dn_b%d" % l, [8 * 128, NPAIR * 128], BF16) for l in range(L)]
    hTT = [[dscr("hT%d_%d" % (par, b), [D, T], F32) for b in range(NSEQ)] for par in range(2)]
    RT = [dscr("RT%d" % b, [512, T], F32) for b in range(NSEQ)]
    GT = [dscr("GT%d" % b, [512, T], F32) for b in range(NSEQ)]
    QgT = [dscr("QgT%d" % b, [2, 128, T], BF16) for b in range(NSEQ)]
    QwT = [dscr("QwT%d" % b, [2, 128, T], BF16) for b in range(NSEQ)]
    KgT = [dscr("KgT%d" % b, [128, T], BF16) for b in range(NSEQ)]
    KwT = [dscr("KwT%d" % b, [128, T], BF16) for b in range(NSEQ)]
    VV = [dscr("VV%d" % b, [T, 256], BF16) for b in range(NSEQ)]
    mixT = [dscr("mixT%d" % b, [D, T], BF16) for b in range(NSEQ)]
    a2T = [dscr("a2T%d" % b, [D, T], BF16) for b in range(NSEQ)]

    class Arena:
        def __init__(self, base):
            self.o = base
            self.hi = base

        def a(self, name, shape, dt):
            nbytes = int(np.prod(shape[1:])) * (4 if dt == F32 else 2)
            self.o = (self.o + 63) // 64 * 64
            t = nc.alloc_sbuf_tensor_at(name, list(shape), dt, offset=self.o)
            self.o += nbytes
            self.hi = max(self.hi, self.o)
            return t

    ar = Arena(16512)
    SM = ar.a("SM", [128, NS], F32)
    CF = ar.a("CF", [128, 256], F32)
    CB = ar.a("CB", [128, 768], BF16)
    DS_SC = ar.a("DS_SC", [128, 8, 4], F32)
    MODV = ar.a("MODV", [128, L, 48, 4], F32)
    A1 = ar.a("A1", [128, L, 8, 4], F32)
    A2 = ar.a("A2", [128, L, 8, 4], F32)
    NSP = ar.a("NSP", [128, L, 8], F32)
    ESK = ar.a("ESK", [128, L, 4], F32)
    NBAX = ar.a("NBAX", [128, L * 16], F32)
    NWU = 7
    WUr = [ar.a("WU%d" % i, [128, 2048], BF16) for i in range(NWU)]
    NWD = 3
    WDr = [ar.a("WD%d" % i, [128, NPAIR, 128], BF16) for i in range(NWD)]
    WBD = ar.a("WBD", [128, 16, 128], BF16)
    big0 = ar.o
    va = Arena(big0)
    Xb = [va.a("X%d" % i, [128, 8, 512], F32) for i in range(2)]
    XS = va.a("XS", [128, 8, 512], BF16)
    Ab = [va.a("A%d" % i, [128, 8, 512], BF16) for i in range(2)]
    MXb = [va.a("MX%d" % i, [128, 8, 512], BF16) for i in range(2)]
    ACTT = va.a("ACTT", [128, NPAIR, 512], BF16)
    WINa = nc.alloc_sbuf_tensor_at("WINa", [128, 4, 2048], BF16, offset=va.o - 22 * 1024 - 16 * 1024)
    WINb = nc.alloc_sbuf_tensor_at("WINb", [128, 4, 2048], BF16, offset=va.o - 22 * 1024)
    NF = 12
    Fb = [va.a("F%d" % i, [128, 512], F32) for i in range(NF)]
    NH = 8
    Hb = [va.a("H%d" % i, [128, 512], BF16) for i in range(NH)]
    RSb = [va.a("RS%d" % i, [128, 512], F32) for i in range(2)]
    CTb = [va.a("CT%d" % i, [128, 512], F32) for i in range(2)]
    STb = [va.a("ST%d" % i, [128, 512], F32) for i in range(2)]
    XINb = [va.a("XIN%d" % i, [128, 1024], F32) for i in range(2)]
    keysAE = (["X0", "X1", "XS", "A0", "A1", "MX0", "MX1", "ACTT", "RS0", "RS1", "CT0", "CT1", "ST0", "ST1",
               "XIN0", "XIN1"] + ["F%d" % i for i in range(NF)] + ["H%d" % i for i in range(NH)])
    vb = Arena(big0)
    Bb = [vb.a("B%d" % i, [128, WB], F32) for i in range(5)]
    UB = vb.a("UB", [128, WB], BF16)
    keysB = ["B0", "B1", "B2", "B3", "B4", "UB"]
    vc = Arena(vb.hi)
    KTb = [vc.a("KT%d" % i, [128, T], BF16) for i in range(1)]
    VAb = [vc.a("VA%d" % i, [128, NKB, 2, 128], BF16) for i in range(1)]
    QTb = [vc.a("QT%d" % i, [128, 2, 512], BF16) for i in range(4)]
    NPT = 4
    PTb = [vc.a("PT%d" % i, [128, 512], BF16) for i in range(NPT)]
    RCb = [vc.a("RC%d" % i, [128, 512], F32) for i in range(2)]
    YBb = [vc.a("YB%d" % i, [128, 512], BF16) for i in range(2)]
    keysCD = (["KT0", "VA0", "QT0", "QT1", "QT2", "QT3", "RC0", "RC1", "YB0", "YB1"] + ["PT%d" % i for i in range(NPT)])
    hi = max(va.hi, vb.hi, vc.hi)
    assert hi <= 229376 - 64, hi
    PS = [nc.alloc_psum_tensor("P%d" % i, [128, 512], F32) for i in range(8)]

    def P(i):
        return ("P", i)

    cur_view = ["AE"]
    view_keys = {"AE": keysAE, "BCD": keysB + keysCD}

    def switch(view):
        if cur_view[0] == view:
            return
        S.handoff(view_keys[cur_view[0]], view_keys[view])
        cur_view[0] = view

    class Ring:
        def __init__(self, names, tiles):
            self.names = names
            self.tiles = tiles
            self.i = 0

        def next(self):
            j = self.i % len(self.tiles)
            self.i += 1
            return self.names[j], self.tiles[j]

    Fr = Ring(["F%d" % i for i in range(NF)], Fb)
    Hr = Ring(["H%d" % i for i in range(NH)], Hb)
    PTr = Ring(["PT%d" % i for i in range(NPT)], PTb)

    def sm(name, idx):
        o = SO[name] + idx
        return SM[:, o:o + 1]

    def sm_rng(name, idx, n):
        o = SO[name] + idx
        return SM[:, o:o + n]

    ident = CF[:, 0:128]
    perm = CF[:, 128:256]
    ones_s = CB[:, 0:128]
    blk_s = CB[:, 128:256]

    def mask(side):
        return CB[:, 256 + side * 256: 256 + side * 256 + 256]

    def mm(out, lhsT, rhs, start, stop, reads, writes):
        S.op("pe", lambda e: e.matmul(out, lhsT=lhsT, rhs=rhs, start=start, stop=stop), reads, writes)

    def act(out, in_, func, reads, writes, bias=None, scale=None, eng="act"):
        kw = {}
        if bias is not None:
            kw["bias"] = bias
        if scale is not None:
            kw["scale"] = scale
        S.op("act", lambda e: e.activation(out=out, in_=in_, func=func, **kw), reads, writes)

    def tt(eng, out, in0, in1, op, reads, writes):
        S.op(eng, lambda e: e.tensor_tensor(out=out, in0=in0, in1=in1, op=op), reads, writes)

    def ts(eng, out, in0, s1, s2, op0, op1, reads, writes):
        if op1 is None:
            S.op(eng, lambda e: e.tensor_scalar(out=out, in0=in0, scalar1=s1, scalar2=None, op0=op0), reads, writes)
        else:
            S.op(eng, lambda e: e.tensor_scalar(out=out, in0=in0, scalar1=s1, scalar2=s2, op0=op0, op1=op1), reads, writes)

    def stt(out, in0, scalar, in1, op0, op1, reads, writes):
        S.op("dve", lambda e: e.scalar_tensor_tensor(out=out, in0=in0, scalar=scalar, in1=in1, op0=op0, op1=op1), reads, writes)

    def recip(out, in_, reads, writes):
        S.op("dve", lambda e: e.reciprocal(out=out, in_=in_), reads, writes)

    def cp(eng, out, in_, reads, writes):
        if eng == "act":
            S.op("act", lambda e: e.activation(out=out, in_=in_, func=AF.Copy), reads, writes)
        else:
            S.op(eng, lambda e: e.tensor_copy(out=out, in_=in_), reads, writes)

    def mset(eng, ap, val, writes):
        S.op(eng, lambda e: e.memset(ap, val), (), writes)

    def load(out, in_, reads, writes, eng="sp"):
        S.dma(eng, lambda e: e.dma_start(out=out, in_=in_), reads, writes)

    def store(out, in_, reads, writes, eng="pool"):
        S.dma(eng, lambda e: e.dma_start(out=out, in_=in_), reads, writes)

    def tiles_AE():
        r = [(0, 0, CTX)]
        for j in range(S_len // 512):
            r.append((1 + j, CTX + 512 * j, 512))
        return r

    def ffn_tiles(need_ctx):
        r = []
        if need_ctx:
            r.append((0, 0, CTX, CTX))
        o0 = 0
        while o0 < S_len:
            n = min(510, S_len - o0)
            r.append((CTX, o0, n, S_len))
            o0 += n
        return r

    def wsrc(spec):
        name, l, idx = spec
        if name == "wout":
            return wout_b[l][idx * 128:(idx + 1) * 128, :]
        if name == "wup":
            return wup_b[l][idx * 128:(idx + 1) * 128, :]
        return wdn_b[l][idx * 128:(idx + 1) * 128, :]

    class WStream:
        def __init__(self):
            self.record = wplan is None
            self.items = [] if self.record else list(wplan)
            self.req = 0
            self.ring = {"wu": (WUr, NWU), "wd": (WDr, NWD)}
            self.cnt = {"wu": 0, "wd": 0}
            self.fifo = {"wu": [], "wd": []}
            self.bykind = {k: [sp for (kk, sp) in self.items if kk == k] for k in ("wu", "wd")}
            self.issued = {"wu": 0, "wd": 0}

        def _issue(self, kind, spec):
            src = wsrc(spec)
            tiles, n = self.ring[kind]
            j = self.cnt[kind] % n
            self.cnt[kind] += 1
            key = "%s%d" % (kind.upper(), j)
            if kind == "wu":
                load(tiles[j][:], src, [("wcv", src.tensor.name)], [key])
            else:
                load(tiles[j][:].rearrange("p i m -> p (i m)"), src, [("wcv", src.tensor.name)], [key])
            self.fifo[kind].append((key, tiles[j]))
            self.issued[kind] += 1

        def _topup(self):
            for k in ("wu", "wd"):
                lst = self.bykind[k]
                _, n = self.ring[k]
                while self.issued[k] < len(lst) and len(self.fifo[k]) < n - 1:
                    self._issue(k, lst[self.issued[k]])

        def get(self, kind, spec):
            if self.record:
                self.items.append((kind, spec))
                self._issue(kind, spec)
                return self.fifo[kind].pop(0)
            assert self.items[self.req] == (kind, spec), (self.req, self.items[self.req], kind, spec)
            self.req += 1
            if not self.fifo[kind]:
                self._topup()
            r = self.fifo[kind].pop(0)
            self._topup()
            return r

    load(SM[:], small_d, [], ["SM"])
    load(CF[:], cf32_d, [], ["CF"])
    load(CB[:], cb16_d, [], ["CB"])

    def convert_weights(l):
        for src, dst in ((win_d[l], win_b[l]), (wout_d[l], wout_b[l]), (wup_d[l], wup_b[l]), (wdn_d[l], wdn_b[l])):
            R = src.shape[0]
            W = src.shape[1]
            step = 1024
            for r0 in range(0, R, step):
                r1 = min(R, r0 + step)
                if W <= 2048:
                    store(dst[r0:r1, :], src[r0:r1, :], [], [("wcv", dst.tensor.name)])
                else:
                    for c0 in range(0, W, 1408):
                        store(dst[r0:r1, c0:c0 + 1408], src[r0:r1, c0:c0 + 1408], [], [("wcv", dst.tensor.name)])

    convert_weights(0)

    def convert_pieces(l):
        for src, dst in ((win_d[l], win_b[l]), (wout_d[l], wout_b[l]), (wup_d[l], wup_b[l]), (wdn_d[l], wdn_b[l])):
            R = src.shape[0]
            W = src.shape[1]
            for r0 in range(0, R, 256):
                r1 = min(R, r0 + 256)
                if W <= 2048:
                    store(dst[r0:r1, :], src[r0:r1, :], [], [("wcv", dst.tensor.name)])
                else:
                    for c0 in range(0, W, 1408):
                        store(dst[r0:r1, c0:c0 + 1408], src[r0:r1, c0:c0 + 1408], [], [("wcv", dst.tensor.name)])
                yield CVT_COST

    act(DS_SC[:].rearrange("p k v -> p (k v)"), sm_rng("cvec", 0, 32), AF.Silu, ["SM"], ["SC"])
    for l in range(L):
        act(NSP[:, l, :], sm_rng("lam", l * 8, 8), AF.Exp, ["SM"], ["NSP"], scale=-1.0)
        act(NSP[:, l, :], NSP[:, l, :], AF.Ln, ["NSP"], ["NSP"], bias=1.0)
        ts("dve", NSP[:, l, :], NSP[:, l, :], -8.0, None, ALU.mult, None, ["NSP"], ["NSP"])
        act(ESK[:, l, :], sm_rng("sink", l * 4, 4), AF.Exp, ["SM"], ["ESK"])

    ts("dve", NBAX[:], sm_rng("bax", 0, L * 16), -1.0, None, ALU.mult, None, ["SM"], ["NBAX"])
    mod_ps = PS[2]
    SCB = Hb[0][:, 0:32].rearrange("p (k v) -> p k v", v=4)
    cp("act", Hb[0][:, 0:32], DS_SC[:].rearrange("p k v -> p (k v)"), ["SC"], ["H0"])
    wm_tiles = [("A0", Ab[0]), ("A1", Ab[1]), ("MX0", MXb[0]), ("MX1", MXb[1])]
    for l in range(L):
        for jg in range(12):
            nm, xt = wm_tiles[jg % 4]
            store(xt[:], wmod_d[l].rearrange("(k p) j -> p k j", p=128)[:, :, jg * 512:(jg + 1) * 512], [], [nm])
            for jj in range(4):
                c = jg * 4 + jj
                for k in range(8):
                    mm(mod_ps[:, c * 4:c * 4 + 4], xt[:, k, jj * 128:(jj + 1) * 128], SCB[:, k, :],
                       k == 0, k == 7, [nm, "H0"], [P(2)])
        for v in range(3):
            tt("dve", MODV[:, l, :, v], mod_ps[:, 0:192].rearrange("p (c v) -> p c v", v=4)[:, :, v],
               sm_rng("bmod", l * 48, 48), ALU.add, [P(2), "SM"], ["MODV"])
        for v in range(3):
            stt(A1[:, l, :, v], MODV[:, l, 8:16, v], 1.0, sm_rng("n1g", l * 8, 8), ALU.add, ALU.mult, ["MODV", "SM"], ["A1"])
            stt(A2[:, l, :, v], MODV[:, l, 32:40, v], 1.0, sm_rng("n2g", l * 8, 8), ALU.add, ALU.mult, ["MODV", "SM"], ["A2"])

    def modsc(l, which, k, v):
        return MODV[:, l, which * 8 + k, v:v + 1]


    def stage0(b):
        xi = 0
        for (ti, t0, n) in tiles_AE():
            nm, xt = "X%d" % (ti % 2), Xb[ti % 2]
            for tb in range(n // 128):
                src = ctx_d[b, tb * 128:(tb + 1) * 128, :] if ti == 0 else x_d[b, (t0 - CTX) + tb * 128:(t0 - CTX) + (tb + 1) * 128, :]
                xn, xin = "XIN%d" % (xi % 2), XINb[xi % 2]
                xi += 1
                load(xin[:], src, [], [xn])
                for half in range(2):
                    pb = half
                    for kk in range(4):
                        k = half * 4 + kk
                        S.op("pe", lambda e, o=PS[pb][:, kk * 128:(kk + 1) * 128], i=xin[:, k * 128:(k + 1) * 128]:
                             e.transpose(out=o, in_=i, identity=ident), [xn, "CF"], [P(pb)])
                    dst = xt[:, half * 4:half * 4 + 4, tb * 128:(tb + 1) * 128]
                    cp("act" if half == 0 else "dve", dst, PS[pb][:, :].rearrange("p (k t) -> p k t", t=128), [P(pb)], [nm])
            store(hTT[0][b].rearrange("(k p) t -> p k t", p=128)[:, :, t0:t0 + n], xt[:, :, 0:n], [nm], [("hT", 0, b, ti)])

    def norm_steps(xn, xt, an, at, n, Acoef, Bcoef_which, l, v, st_bank=0):
        rn, rs = "RS0", RSb[0]
        steps = []

        def s_sq():
            act(XS[:, :, 0:n], xt[:, :, 0:n], AF.Square, [xn], ["XS"])

        def s_mm():
            for k in range(8):
                mm(PS[st_bank][:, 0:n], ones_s, XS[:, k, 0:n], k == 0, k == 7, ["XS", "CB"], [P(st_bank)])

        def s_rs():
            act(rs[:, 0:n], PS[st_bank][:, 0:n], AF.Ln, [P(st_bank)], [rn], bias=EPS)
            act(rs[:, 0:n], rs[:, 0:n], AF.Exp, [rn], [rn], scale=-0.5)

        def s_k(k):
            fn, ft = Fr.next()
            stt(ft[:, 0:n], xt[:, k, 0:n], Acoef[:, l, k, v:v + 1], rs[:, 0:n], ALU.mult, ALU.mult, [xn, rn, "A1", "A2"], [fn])
            act(at[:, k, 0:n], ft[:, 0:n], AF.Identity, [fn, "MODV"], [an], bias=modsc(l, Bcoef_which, k, v))

        steps += [s_sq, s_mm, s_rs]
        for k in range(8):
            steps.append(lambda k=k: s_k(k))
        return steps

    def run_steps(steps, k=None):
        m = len(steps) if k is None else min(k, len(steps))
        for _ in range(m):
            steps.pop(0)()

    qpend = []

    def advance_q(flush=False):
        while True:
            for st in list(qpend):
                st.pop(0)()
                if not st:
                    qpend.remove(st)
            if not flush or not qpend:
                break

    def qk_post(pb, n, is_lat, gain, dst, tl0, ctn, ctt, stn, stt_, wkeys):
        hn, ht = Hr.next()
        st = {}
        steps = []
        if gain is not None:
            sn, sq = Hr.next()
            rawn, raw = Fr.next()
            cp("act", raw[:, 0:n], PS[pb][:, 0:n], [P(pb)], [rawn])
            act(sq[:, 0:n], raw[:, 0:n], AF.Square, [rawn], [sn])
            rn, rs = "RS1", RSb[1]

            def s2():
                mm(PS[1][:, 0:n], blk_s, sq[:, 0:n], True, True, [sn, "CB"], [P(1)])
                act(rs[:, 0:n], PS[1][:, 0:n], AF.Ln, [P(1)], [rn], bias=EPS)
                act(rs[:, 0:n], rs[:, 0:n], AF.Exp, [rn], [rn], scale=-0.5)
                if is_lat:
                    st["f"] = Fr.next()
                    stt(st["f"][1][:, 0:n], raw[:, 0:n], gain, rs[:, 0:n], ALU.mult, ALU.mult, [rawn, rn, "SM"], [st["f"][0]])
                else:
                    stt(ht[:, 0:n], raw[:, 0:n], gain, rs[:, 0:n], ALU.mult, ALU.mult, [rawn, rn, "SM"], [hn])
                    store(dst, ht[:, 0:n], [hn], wkeys)

            steps.append(s2)
        else:
            if is_lat:
                st["f"] = Fr.next()
                cp("act", st["f"][1][:, 0:n], PS[pb][:, 0:n], [P(pb)], [st["f"][0]])
            else:
                cp("act", ht[:, 0:n], PS[pb][:, 0:n], [P(pb)], [hn])
                store(dst, ht[:, 0:n], [hn], wkeys)
        if is_lat:
            def s3():
                fn, ft = st["f"]
                mm(PS[2][:, 0:n], perm, ft[:, 0:n], True, True, [fn, "CF"], [P(2)])
                f2n, f2 = Fr.next()
                tt("dve", f2[:, 0:n], PS[2][:, 0:n], stt_[:, 0:n], ALU.mult, [P(2), stn], [f2n])
                f3n, f3 = Fr.next()
                tt("pool", f3[:, 0:n], ft[:, 0:n], ctt[:, 0:n], ALU.mult, [fn, ctn], [f3n])
                tt("dve", ht[:, 0:n], f2[:, 0:n], f3[:, 0:n], ALU.add, [f2n, f3n], [hn])
                store(dst, ht[:, 0:n], [hn], wkeys)

            steps.append(s3)
        if steps:
            qpend.append(steps)

    def load_win(l, half):
        src = win_b[l].rearrange("(u p) f -> p u f", p=128)[:, half * 4:half * 4 + 4, :]
        if half == 0:
            load(WINa[:], src, [("wcv", win_b[l].tensor.name)], ["MX0", "MX1"])
        else:
            load(WINb[:], src, [("wcv", win_b[l].tensor.name)], ["ACTT"])

    def stageA(l, b, ws, first):
        tl = tiles_AE()
        if b == 0:
            if first:
                load_win(l, 0)
            load_win(l, 1)

        def prep(ti, t0, n):
            is_lat = ti > 0
            v = b if is_lat else NSEQ
            xn, xt = "X%d" % (ti % 2), Xb[ti % 2]
            an, at = "A%d" % (ti % 2), Ab[ti % 2]
            ctn, ctt = "CT%d" % (ti % 2), CTb[ti % 2]
            stn, stt_ = "ST%d" % (ti % 2), STb[ti % 2]

            def s_load():
                load(xt[:, :, 0:n], hTT[l % 2][b].rearrange("(k p) t -> p k t", p=128)[:, :, t0:t0 + n], [("hT", l % 2, b, ti)], [xn])
                if is_lat:
                    load(ctt[:, 0:n], rope_d[0, :, t0 - CTX:t0 - CTX + n], [], [ctn])
                    load(stt_[:, 0:n], rope_d[1, :, t0 - CTX:t0 - CTX + n], [], [stn])

            return [s_load] + norm_steps(xn, xt, an, at, n, A1, 0, l, v)

        nxt = prep(*tl[0])
        for idx, (ti, t0, n) in enumerate(tl):
            run_steps(nxt)
            nxt = prep(*tl[idx + 1]) if idx + 1 < len(tl) else []
            is_lat = ti > 0
            v = b if is_lat else NSEQ
            xn, xt = "X%d" % (ti % 2), Xb[ti % 2]
            an, at = "A%d" % (ti % 2), Ab[ti % 2]
            ctn, ctt = "CT%d" % (ti % 2), CTb[ti % 2]
            stn, stt_ = "ST%d" % (ti % 2), STb[ti % 2]
            pj = 0
            for u in range(8):
                if u >= 1:
                    run_steps(nxt, 2)
                if u < 4:
                    wk, w3 = ["MX0", "MX1"], WINa[:, u, :].rearrange("p (k c) -> p k c", k=8)
                else:
                    wk, w3 = ["ACTT"], WINb[:, u - 4, :].rearrange("p (k c) -> p k c", k=8)
                if u < 7:
                    for half in range(2):
                        ci = 2 * u + half
                        pb = 3 + (pj % 4)
                        pj += 1
                        for k in range(8):
                            mm(PS[pb][:, 0:n], w3[:, k, half * 128:(half + 1) * 128], at[:, k, 0:n], k == 0, k == 7, wk + [an], [P(pb)])
                        advance_q()
                        if ci < 4:
                            fn, ft = Fr.next()
                            cp("act", ft[:, 0:n], PS[pb][:, 0:n], [P(pb)], [fn])
                            store(RT[b][ci * 128:(ci + 1) * 128, t0:t0 + n], ft[:, 0:n], [fn], [("RT", b, ci)])
                        elif ci < 8:
                            fn, ft = Fr.next()
                            act(ft[:, 0:n], PS[pb][:, 0:n], AF.Gelu_apprx_tanh, [P(pb)], [fn])
                            store(GT[b][(ci - 4) * 128:(ci - 3) * 128, t0:t0 + n], ft[:, 0:n], [fn], [("GT", b, ci - 4)])
                        elif ci in (8, 9):
                            qk_post(pb, n, is_lat, sm("qg", l * 2 + 0), QgT[b][ci - 8, :, t0:t0 + n], t0 - CTX, ctn, ctt, stn, stt_, [("QgT", b)])
                        elif ci == 10:
                            qk_post(pb, n, is_lat, sm("qg", l * 2 + 1), KgT[b][:, t0:t0 + n], t0 - CTX, ctn, ctt, stn, stt_, [("KgT", b)])
                        elif ci in (11, 12):
                            qk_post(pb, n, is_lat, None, QwT[b][ci - 11, :, t0:t0 + n], t0 - CTX, ctn, ctt, stn, stt_, [("QwT", b)])
                        else:
                            qk_post(pb, n, is_lat, None, KwT[b][:, t0:t0 + n], t0 - CTX, ctn, ctt, stn, stt_, [("KwT", b)])
                else:
                    for tb in range(n // 128):
                        pb = 7
                        for k in range(8):
                            mm(PS[pb][:, 0:256], at[:, k, tb * 128:(tb + 1) * 128], w3[:, k, :], k == 0, k == 7, wk + [an], [P(pb)])
                        hn, ht = Hr.next()
                        cp("act" if tb % 2 else "dve", ht[:, 0:256], PS[pb][:, 0:256], [P(pb)], [hn])
                        store(VV[b][t0 + tb * 128:t0 + (tb + 1) * 128, :], ht[:, 0:256], [hn], [("VV", b)])
                        if tb % 2 == 1:
                            advance_q()
        advance_q(flush=True)

    def rev_ap(t, c0, n):
        a = t[:, c0:c0 + n]
        return bass.AP(a.tensor, a.offset + n - 1, [list(a.ap[0]), [-1, n]])

    def stageB(l, b):
        Rp, U, A_, D_, Hf = Bb
        for c in range(4):
            mset("pool", Rp[:, 0:2], 0.0, ["B0"])
            mset("pool", Rp[:, 258:261], 0.0, ["B0"])
            mset("pool", Rp[:, 261 + S_len:262 + S_len], 0.0, ["B0"])
            load(Rp[:, 2:258], RT[b][c * 128:(c + 1) * 128, 0:CTX], [("RT", b, c)], ["B0"])
            load(Rp[:, 261:261 + S_len], RT[b][c * 128:(c + 1) * 128, CTX:T], [("RT", b, c)], ["B0"])
            cwo = l * 16 + c * 4
            ts("dve", U[:, 0:WU], Rp[:, 0:WU], sm("cw", cwo), sm("cb", l * 4 + c), ALU.mult, ALU.add, ["B0", "SM"], ["B1"])
            for k in range(1, 4):
                stt(U[:, 0:WU], Rp[:, k:k + WU], sm("cw", cwo + k), U[:, 0:WU], ALU.mult, ALU.add, ["B0", "B1", "SM"], ["B1"])
            yield 18.0
            cp("pool", UB[:, 0:WU], U[:, 0:WU], ["B1"], ["UB"])
            yield 6.0
            for d in range(2):
                pbi = 0
                for j0 in range(0, WU, 512):
                    n = min(512, WU - j0)
                    for gate, dstt, dn in ((0, A_, "B2"), (1, D_, "B3")):
                        pb = 6 + pbi % 2
                        pbi += 1
                        mm(PS[pb][:, 0:n], WBD[:, (gate * 2 + d) * 4 + c, :], UB[:, j0:j0 + n], True, True, ["UB", "WBD"], [P(pb)])
                        act(dstt[:, j0:j0 + n], PS[pb][:, 0:n], AF.Sigmoid, [P(pb), "SM"], [dn],
                            bias=sm("bax", l * 16 + (gate * 2 + d) * 4 + c))
                yield 22.0
                act(A_[:, 0:WU], A_[:, 0:WU], AF.Exp, ["B2", "NSP"], ["B2"], scale=NSP[:, l, d * 4 + c:d * 4 + c + 1])
                yield 3.0
                tt("dve", D_[:, 0:WU], D_[:, 0:WU], U[:, 0:WU], ALU.mult, ["B3", "B1"], ["B3"])
                tt("pool", Rp[:, 0:WU], A_[:, 0:WU], A_[:, 0:WU], ALU.mult, ["B2"], ["B0"])
                yield 4.5
                act(Rp[:, 0:WU], Rp[:, 0:WU], AF.Ln, ["B0"], ["B0"], scale=-1.0, bias=1.0)
                act(Rp[:, 0:WU], Rp[:, 0:WU], AF.Exp, ["B0"], ["B0"], scale=0.5)
                yield 6.0
                tt("dve", D_[:, 0:WU], D_[:, 0:WU], Rp[:, 0:WU], ALU.mult, ["B3", "B0"], ["B3"])
                yield 4.5
                if d == 0:
                    mset("dve", A_[:, 256:259], 1.0, ["B2"])
                    mset("dve", D_[:, 256:259], 0.0, ["B3"])
                    S.op("dve", lambda e: e.tensor_tensor_scan(out=Hf[:, 0:WU], data0=A_[:, 0:WU], data1=D_[:, 0:WU], initial=0.0,
                                                               op0=ALU.mult, op1=ALU.add), ["B2", "B3"], ["B4"])
                else:
                    S.op("dve", lambda e: e.tensor_tensor_scan(out=rev_ap(Rp, 0, CTX), data0=rev_ap(A_, 0, CTX), data1=rev_ap(D_, 0, CTX),
                                                               initial=0.0, op0=ALU.mult, op1=ALU.add), ["B2", "B3"], ["B0"])
                    S.op("dve", lambda e: e.tensor_tensor_scan(out=rev_ap(Rp, 259, S_len), data0=rev_ap(A_, 259, S_len),
                                                               data1=rev_ap(D_, 259, S_len), initial=Rp[:, 0:1],
                                                               op0=ALU.mult, op1=ALU.add), ["B2", "B3", "B0"], ["B0"])
                yield 9.0
            load(A_[:, 0:CTX], GT[b][c * 128:(c + 1) * 128, 0:CTX], [("GT", b, c)], ["B2"])
            load(A_[:, 259:259 + S_len], GT[b][c * 128:(c + 1) * 128, CTX:T], [("GT", b, c)], ["B2"])
            tt("dve", Hf[:, 0:CTX], Hf[:, 0:CTX], Rp[:, 0:CTX], ALU.add, ["B4", "B0"], ["B4"])
            tt("dve", Hf[:, 259:WU], Hf[:, 259:WU], Rp[:, 259:WU], ALU.add, ["B4", "B0"], ["B4"])
            tt("pool", UB[:, 0:CTX], Hf[:, 0:CTX], A_[:, 0:CTX], ALU.mult, ["B4", "B2"], ["UB"])
            tt("pool", UB[:, 259:WU], Hf[:, 259:WU], A_[:, 259:WU], ALU.mult, ["B4", "B2"], ["UB"])
            store(mixT[b][c * 128:(c + 1) * 128, 0:CTX], UB[:, 0:CTX], ["UB"], [("mixT", b)])
            store(mixT[b][c * 128:(c + 1) * 128, CTX:T], UB[:, 259:WU], ["UB"], [("mixT", b)])
            yield 12.0

    def load_kv(b, which, Ksrc, vcol0, kkey, slot):
        kn, kt = "KT%d" % slot, KTb[slot]
        vn, vt = "VA%d" % slot, VAb[slot]
        load(kt[:], Ksrc[b][:, :], [(kkey, b)], [kn])
        for g in range(2):
            load(vt[:, :, g, 0:64], VV[b].rearrange("(kb p) c -> p kb c", p=128)[:, :, vcol0 + g * 64:vcol0 + g * 64 + 64],
                 [("VV", b)], [vn])
        return kn, kt, vn, vt

    def attention(l, b, need_ctx, window):
        slot = 0
        mset("pool", VAb[slot][:, :, :, 64:128], 1.0, ["VA%d" % slot])
        for j in range(2):
            mset("pool", QTb[2 * j + 0][64:128, :, :], 0.0, ["QT%d" % (2 * j)])
            mset("pool", QTb[2 * j + 1][0:64, :, :], 0.0, ["QT%d" % (2 * j + 1)])
        if window:
            kn, kt, vn, vt = load_kv(b, 1, KwT, 128, "KwT", slot)
            Qsrc, qkey, row0 = QwT, "QwT", 768
        else:
            kn, kt, vn, vt = load_kv(b, 0, KgT, 0, "KgT", slot)
            Qsrc, qkey, row0 = QgT, "QgT", 512
        groups = []
        if window:
            if need_ctx:
                for qb in range(2):
                    for g in range(2):
                        groups.append((qb * 128, 128, g, [(0, None), (1, None)]))
            for qb in range(NB):
                kbs = [(0, None), (1, None)]
                if qb > 0:
                    kbs.append((2 + qb - 1, 0))
                kbs.append((2 + qb, None))
                if qb < NB - 1:
                    kbs.append((2 + qb + 1, 1))
                for g in range(2):
                    groups.append((CTX + qb * 128, 128, g, kbs))
        else:
            if need_ctx:
                for g in range(2):
                    groups.append((0, 256, g, [(0, None), (1, None)]))
            for qt in range(S_len // 256):
                for g in range(2):
                    groups.append((CTX + qt * 256, 256, g, [(kb, None) for kb in range(NKB)]))
        steps = []
        for gi, (q0, nq, g, kbs) in enumerate(groups):
            for si, (kb, mk) in enumerate(kbs):
                steps.append((gi, si, kb, mk, si == 0, si == len(kbs) - 1))
        qt_loaded = {}
        qstate = {"i": 0, "cur": None}

        def q_tile(q0, nq):
            base = (q0 // 512) * 512 if q0 >= CTX else 0
            if q0 >= CTX:
                base = CTX + ((q0 - CTX) // 512) * 512
                width = min(512, T - base)
            else:
                base, width = 0, CTX
            if qstate["cur"] is None or qstate["cur"][0] != base:
                j = qstate["i"] % 2
                qstate["i"] += 1
                for gg in range(2):
                    load(QTb[2 * j + gg][64 * gg:64 * gg + 64, :, 0:width],
                         Qsrc[b].rearrange("x p t -> p x t")[64 * gg:64 * gg + 64, :, base:base + width], [(qkey, b)], ["QT%d" % (2 * j + gg)])
                qstate["cur"] = (base, j)
            base, j = qstate["cur"]
            return j, q0 - base

        sc_banks = [0, 1, 2, 5]
        oc_banks = [3, 4]
        sc_i = [0]
        LA = 3
        pend = []
        scale = 0.125

        def issue_S(st):
            gi, si, kb, mk, first, last = st
            q0, nq, g, kbs = groups[gi]
            j, qo = q_tile(q0, nq)
            qn, qv = "QT%d" % (2 * j + g), QTb[2 * j + g][:, :, qo:qo + nq]
            pb = sc_banks[sc_i[0] % 4]
            sc_i[0] += 1
            N = 2 * nq
            mm(PS[pb][:, 0:N], kt[:, kb * 128:(kb + 1) * 128], qv, True, True, [kn, qn], [P(pb)])
            pend.append((st, pb, N))

        def finish(stp):
            st, pb, N = stp
            gi, si, kb, mk, first, last = st
            q0, nq, g, kbs = groups[gi]
            ob = oc_banks[gi % 2]
            pn, pt = PTr.next()
            act(pt[:, 0:N], PS[pb][:, 0:N], AF.Exp, [P(pb)], [pn], scale=scale)
            if mk is not None:
                tt("dve", pt[:, 0:N], pt[:, 0:N], mask(mk), ALU.mult, [pn, "CB"], [pn])
            mm(PS[ob][:, 0:N], vt[:, kb, g, :], pt[:, 0:N], first, last, [vn, pn], [P(ob)])
            if last:
                rn, rc = "RC%d" % (gi % 2), RCb[gi % 2]
                yn, yb = "YB%d" % (gi % 2), YBb[gi % 2]
                if window:
                    for r in range(2):
                        act(rc[64:128, r * nq:(r + 1) * nq], PS[ob][64:128, r * nq:(r + 1) * nq], AF.Ln, [P(ob), "ESK"], [rn],
                            bias=ESK[64:128, l, 2 * g + r:2 * g + r + 1])
                    act(rc[64:128, 0:N], rc[64:128, 0:N], AF.Exp, [rn], [rn], scale=-1.0)
                else:
                    recip(rc[64:128, 0:N], PS[ob][64:128, 0:N], [P(ob)], [rn])
                tt("dve", yb[0:64, 0:N], PS[ob][0:64, 0:N], rc[64:128, 0:N], ALU.mult, [P(ob), rn], [yn])
                r0 = row0 + 2 * g * 64
                store(mixT[b][r0:r0 + 128, q0:q0 + nq].rearrange("(r d) q -> d r q", d=64),
                      yb[0:64, 0:N].rearrange("d (r q) -> d r q", r=2), [yn], [("mixT", b)])

        cost = 0.35 if window else 0.55
        for i, st in enumerate(steps):
            issue_S(st)
            if len(pend) > LA:
                finish(pend.pop(0))
                yield cost
        while pend:
            finish(pend.pop(0))
            yield cost

    def cover(seg_base, c0, c1):
        if seg_base == 0:
            return [0]
        r = []
        for (ti, t0, n) in tiles_AE():
            if ti == 0:
                continue
            if t0 < c1 and c0 < t0 + n:
                r.append(ti)
        return r

    def stageE(l, b, ws, need_ctx, after_last_op):
        switch("AE")
        pin, pout = l % 2, (l + 1) % 2
        ft_list = ffn_tiles(need_ctx)

        def geom(fi):
            seg, o0, n_out, seg_len = ft_list[fi]
            N = n_out + 2
            lo = max(o0 - 1, 0)
            hi_ = min(o0 + n_out + 1, seg_len)
            c0 = lo - (o0 - 1)
            c1 = c0 + (hi_ - lo)
            v = b if seg > 0 else NSEQ
            return seg, o0, n_out, seg_len, N, lo, hi_, c0, c1, v

        def op_steps(fi):
            seg, o0, n_out, seg_len, N, lo, hi_, c0, c1, v = geom(fi)
            xn, xt = "X%d" % (fi % 2), Xb[fi % 2]
            an, at = "A%d" % (fi % 2), Ab[fi % 2]
            mn, mt = "MX%d" % (fi % 2), MXb[fi % 2]
            tis = cover(seg, seg + lo, seg + hi_)
            steps = []

            def s_load():
                load(xt[:, :, c0:c1], hTT[pin][b].rearrange("(k p) t -> p k t", p=128)[:, :, seg + lo:seg + hi_],
                     [("hT", pin, b, ti) for ti in tis], [xn])
                load(mt[:, :, c0:c1], mixT[b].rearrange("(k p) t -> p k t", p=128)[:, :, seg + lo:seg + hi_], [("mixT", b)], [mn])
                if c0 > 0:
                    mset("pool", at[:, :, 0:1], 0.0, [an])
                if c1 < N:
                    mset("pool", at[:, :, N - 1:N], 0.0, [an])

            steps.append(s_load)

            def s_unit(u):
                wk, wt = ws.get("wu", ("wout", l, u))
                w3 = wt[:].rearrange("p (k c) -> p k c", k=8)
                for half in range(2):
                    m = 2 * u + half
                    pb = 6 + half
                    for k in range(8):
                        mm(PS[pb][:, c0:c1], w3[:, k, half * 128:(half + 1) * 128], mt[:, k, c0:c1], k == 0, k == 7, [wk, mn], [P(pb)])
                    stt(xt[:, m, c0:c1], PS[pb][:, c0:c1], modsc(l, 2, m, v), xt[:, m, c0:c1], ALU.mult, ALU.add, [P(pb), xn, "MODV"], [xn])

            for u in range(4):
                steps.append(lambda u=u: s_unit(u))
            steps += norm_steps(xn, xt[:, :, c0:c1], an, at[:, :, c0:c1], c1 - c0, A2, 3, l, v, st_bank=2)
            return steps

        nxt = op_steps(0)
        run_steps(nxt)
        for fi in range(len(ft_list)):
            nxt = op_steps(fi + 1) if fi + 1 < len(ft_list) else []
            if not nxt and after_last_op is not None:
                after_last_op()
            seg, o0, n_out, seg_len, N, lo, hi_, c0, c1, v = geom(fi)
            xn, xt = "X%d" % (fi % 2), Xb[fi % 2]
            an, at = "A%d" % (fi % 2), Ab[fi % 2]
            prev_tail = None
            for i in range(NPAIR):
                wk, wt = ws.get("wu", ("wup", l, i))
                w4 = wt[:].rearrange("p (g k m) -> p g k m", g=2, k=8)
                pg = (2 * i) % 6
                pv = (2 * i + 1) % 6
                for k in range(8):
                    mm(PS[pg][:, 0:N], w4[:, 0, k, :], at[:, k, 0:N], k == 0, k == 7, [wk, an], [P(pg)])
                for k in range(8):
                    mm(PS[pv][:, 0:N], w4[:, 1, k, :], at[:, k, 0:N], k == 0, k == 7, [wk, an], [P(pv)])
                if i == 2 and nxt:
                    run_steps(nxt, 1)
                outs = []
                for (pb, ch) in ((pg, i), (pv, NPAIR + i)):
                    fn, ft = Fr.next()
                    fo = (l * 44 + ch) * 3
                    act(ft[:, 0:n_out], PS[pb][:, 0:n_out], AF.Identity, [P(pb), "SM"], [fn], scale=sm("fw", fo), bias=sm("fb", l * 44 + ch))
                    outs.append((fn, ft, pb, fo))
                for (fn, ft, pb, fo) in outs:
                    stt(ft[:, 0:n_out], PS[pb][:, 1:n_out + 1], sm("fw", fo + 1), ft[:, 0:n_out], ALU.mult, ALU.add, [P(pb), fn, "SM"], [fn])
                    stt(ft[:, 0:n_out], PS[pb][:, 2:n_out + 2], sm("fw", fo + 2), ft[:, 0:n_out], ALU.mult, ALU.add, [P(pb), fn, "SM"], [fn])
                if prev_tail is not None:
                    prev_tail()

                def tail(i=i, outs=outs):
                    (gn, gt, _, _), (vn_, vt_, _, _) = outs
                    sn, st_ = Fr.next()
                    act(st_[:, 0:n_out], gt[:, 0:n_out], AF.Silu, [gn], [sn])
                    tt("pool", ACTT[:, i, 0:n_out], st_[:, 0:n_out], vt_[:, 0:n_out], ALU.mult, [sn, vn_], ["ACTT"])

                prev_tail = tail
            prev_tail()
            if nxt:
                run_steps(nxt, 4)
                run_steps(nxt, 3)
            tis_o = cover(seg, seg + o0, seg + o0 + n_out)
            for m in range(8):
                wk, wt = ws.get("wd", ("wdn", l, m))
                pb = 6 + (m % 2)
                for i in range(NPAIR):
                    mm(PS[pb][:, 0:n_out], wt[:, i, :], ACTT[:, i, 0:n_out], i == 0, i == NPAIR - 1, [wk, "ACTT"], [P(pb)])
                stt(xt[:, m, 1:1 + n_out], PS[pb][:, 0:n_out], modsc(l, 5, m, v), xt[:, m, 1:1 + n_out], ALU.mult, ALU.add,
                    [P(pb), xn, "MODV"], [xn])
                run_steps(nxt, 1)
            run_steps(nxt)
            store(hTT[pout][b].rearrange("(k p) t -> p k t", p=128)[:, :, seg + o0:seg + o0 + n_out], xt[:, :, 1:1 + n_out], [xn],
                  [("hT", pout, b, ti) for ti in tis_o])

    def stageF(b):
        switch("AE")
        xi = 0
        for (ti, t0, n) in tiles_AE():
            if ti == 0:
                continue
            xn, xt = "X%d" % (ti % 2), Xb[ti % 2]
            load(xt[:, :, 0:n], hTT[L % 2][b].rearrange("(k p) t -> p k t", p=128)[:, :, t0:t0 + n], [("hT", L % 2, b, ti)], [xn])
            act(XS[:, :, 0:n], xt[:, :, 0:n], AF.Square, [xn], ["XS"])
            for k in range(8):
                mm(PS[0][:, 0:n], ones_s, XS[:, k, 0:n], k == 0, k == 7, ["XS", "CB"], [P(0)])
            rn, rs = "RS0", RSb[0]
            act(rs[:, 0:n], PS[0][:, 0:n], AF.Ln, [P(0)], [rn], bias=EPS)
            act(rs[:, 0:n], rs[:, 0:n], AF.Exp, [rn], [rn], scale=-0.5)
            for k in range(8):
                stt(xt[:, k, 0:n], xt[:, k, 0:n], sm("fng", k), rs[:, 0:n], ALU.mult, ALU.mult, [xn, rn, "SM"], [xn])
            for tb in range(n // 128):
                on, ot = "XIN%d" % (xi % 2), XINb[xi % 2]
                xi += 1
                for half in range(2):
                    pb = 1 + half
                    for kk in range(4):
                        k = half * 4 + kk
                        S.op("pe", lambda e, o=PS[pb][:, kk * 128:(kk + 1) * 128], i=xt[:, k, tb * 128:(tb + 1) * 128]:
                             e.transpose(out=o, in_=i, identity=ident), [xn, "CF"], [P(pb)])
                    cp("act" if half == 0 else "dve", ot[:, half * 512:(half + 1) * 512], PS[pb][:, :], [P(pb)], [on])
                r0 = (t0 - CTX) + tb * 128
                store(out_d[b, r0:r0 + 128, :], ot[:], [on], [("out", b)])

    ws = WStream()
    for b in range(NSEQ):
        stage0(b)
    for l in range(L):
        need_ctx = l < L - 1
        store(WBD[:].rearrange("p a m -> p (a m)"), wbd_d[l], [], ["WBD"])
        for b in range(NSEQ):
            stageA(l, b, ws, first=(l == 0 and b == 0))
        cvt = convert_pieces(l + 1) if l + 1 < L else iter(())
        for b in range(NSEQ):
            switch("BCD")

            def att_chain(l=l, b=b, need_ctx=need_ctx):
                yield from attention(l, b, need_ctx, window=False)
                yield from attention(l, b, need_ctx, window=True)

            gens = [[stageB(l, b), 0.0, True], [att_chain(), ATT_DELAY, True], [cvt, 30.0, True]]
            while any(g[2] for g in gens[:2]):
                live = [g for g in gens if g[2]]
                g = min(live, key=lambda t: t[1])
                try:
                    g[1] += next(g[0]) * (B_SCALE if g is gens[0] else 1.0)
                except StopIteration:
                    g[2] = False
        for _ in cvt:
            pass
        for b in range(NSEQ):
            last = (b == NSEQ - 1) and (l + 1 < L)
            stageE(l, b, ws, need_ctx, (lambda nl=l + 1: load_win(nl, 0)) if last else None)
    for b in range(NSEQ):
        stageF(b)
    S.wait_all("sp")
    S.emit()
    S.close()
    build.stats = (S.n_ins, S.n_wait, hi)
    build.wplan = list(ws.items)
    return nc


def fm(vec):
    v = np.asarray(vec, np.float32)
    lead = v.shape[:-1]
    k = v.shape[-1] // 128
    v = v.reshape(lead + (k, 128))
    return np.moveaxis(v, -1, 0)


def rope_tables(S_len):
    n_freq = 16
    t = np.arange(S_len)
    row = (t // 64).astype(np.float32)
    col = (t % 64).astype(np.float32)
    inv = (np.float32(10000.0) ** (-np.arange(n_freq, dtype=np.float32) / np.float32(n_freq))).astype(np.float32)
    ang = np.concatenate([row[:, None] * inv, col[:, None] * inv], axis=-1).astype(np.float32)
    cos = np.cos(ang).astype(np.float32)
    sin = np.sin(ang).astype(np.float32)
    p = np.arange(128)
    d = p % 64
    j = d % 32
    sign = np.where(d < 32, -1.0, 1.0).astype(np.float32)
    C = cos[:, j].T
    Sg = (sin[:, j] * sign[None, :]).T
    return np.ascontiguousarray(np.stack([C, Sg]).astype(np.float32))


def shared_inputs(inp, S_len, L):
    SO, NS = small_layout(L)
    d = {}
    perm_cols = w_in_perm()
    for l in range(L):
        w = np.asarray(inp["w_in"][l], np.float32)[:, perm_cols]
        w = w.reshape(8, 128, 8, 256)
        d["w_in%d" % l] = np.ascontiguousarray(w.transpose(2, 1, 0, 3).reshape(1024, 2048))
        w = np.asarray(inp["w_out"][l], np.float32).reshape(8, 128, 4, 256)
        d["w_out%d" % l] = np.ascontiguousarray(w.transpose(2, 1, 0, 3).reshape(512, 2048))
        w = np.asarray(inp["w_up"][l], np.float32).reshape(8, 128, 2, NPAIR, 128)
        d["w_up%d" % l] = np.ascontiguousarray(w.transpose(3, 1, 2, 0, 4).reshape(NPAIR * 128, 2048))
        w = np.asarray(inp["w_down"][l], np.float32).reshape(NPAIR, 128, 8, 128)
        d["w_dn%d" % l] = np.ascontiguousarray(w.transpose(2, 1, 0, 3).reshape(8 * 128, NPAIR * 128))
        bd = np.zeros((128, 16, 128), np.float32)
        for gate, key in ((0, "lru_w_a"), (1, "lru_w_x")):
            wg = np.asarray(inp[key][l], np.float32)
            for dd in range(2):
                for c in range(4):
                    a = (gate * 2 + dd) * 4 + c
                    for nl in range(2):
                        bd[nl * 64:(nl + 1) * 64, a, nl * 64:(nl + 1) * 64] = wg[dd, 2 * c + nl]
        d["w_bd%d" % l] = bd.reshape(128, 2048)
    d["w_mod"] = np.ascontiguousarray(np.asarray(inp["w_mod"], np.float32)[:L])
    cf = np.zeros((128, 256), np.float32)
    cf[:, 0:128] = np.eye(128, dtype=np.float32)
    for m in range(128):
        src = m + 32 if (m % 64) < 32 else m - 32
        cf[src, 128 + m] = 1.0
    d["cf32"] = cf
    cb = np.zeros((128, 768), np.float32)
    cb[:, 0:128] = 1.0 / 1024.0
    for h in range(2):
        cb[h * 64:(h + 1) * 64, 128 + h * 64:128 + (h + 1) * 64] = 1.0 / 64.0
    k = np.arange(128)[:, None]
    q = np.arange(128)[None, :]
    ml = (k >= q).astype(np.float32)
    mr = (k <= q).astype(np.float32)
    cb[:, 256:384] = ml
    cb[:, 384:512] = ml
    cb[:, 512:640] = mr
    cb[:, 640:768] = mr
    d["cb16"] = cb.astype(ml_dtypes.bfloat16)
    d["rope"] = rope_tables(S_len)
    return d


def small_pack(inp, L, cvecs):
    SO, NS = small_layout(L)
    sm = np.zeros((128, NS), np.float32)

    def put(name, arr):
        a = np.asarray(arr, np.float32).reshape(128, -1)
        sm[:, SO[name]:SO[name] + a.shape[1]] = a

    put("bmod", fm(np.asarray(inp["b_mod"])[:L]))
    put("n1g", fm(np.asarray(inp["norm1_g"])[:L]))
    put("n2g", fm(np.asarray(inp["norm2_g"])[:L]))
    put("fng", fm(np.asarray(inp["final_norm_g"])))
    cw = fm(np.asarray(inp["lru_conv_w"])[:L])
    put("cw", cw.transpose(0, 1, 3, 2))
    put("cb", fm(np.asarray(inp["lru_conv_b"])[:L]))
    ba = fm(np.asarray(inp["lru_b_a"])[:L])
    bx = fm(np.asarray(inp["lru_b_x"])[:L])
    put("bax", np.stack([ba, bx], axis=2))
    put("lam", fm(np.asarray(inp["lru_lam"])[:L]))
    qg = np.stack([np.asarray(inp["ga_q_norm_g"])[:L], np.asarray(inp["ga_k_norm_g"])[:L]], axis=1)
    qg = np.concatenate([qg, qg], axis=-1)
    put("qg", np.moveaxis(qg, -1, 0))
    sk = np.asarray(inp["wa_sink"], np.float32)[:L]
    put("sink", np.broadcast_to(sk[None], (128, L, 4)))
    fw = fm(np.asarray(inp["ffn_conv_w"])[:L])
    put("fw", fw.transpose(0, 1, 3, 2))
    put("fb", fm(np.asarray(inp["ffn_conv_b"])[:L]))
    cv = np.zeros((128, 8, 4), np.float32)
    for i, c in enumerate(cvecs):
        cv[:, :, i] = fm(c)
    put("cvec", cv)
    return sm


_CACHE = {}


def run(inp, S_len, NSEQ, L, n_cores, trace=False):
    key = (S_len, NSEQ, L)
    if key not in _CACHE:
        build(S_len, NSEQ, L)
        _CACHE[key] = build(S_len, NSEQ, L, wplan=build.wplan)
    nc = _CACHE[key]
    shared = shared_inputs(inp, S_len, L)
    x = np.asarray(inp["x"], np.float32)
    ctx = np.asarray(inp["ctx"], np.float32)
    c = np.asarray(inp["c"], np.float32)
    c_ctx = np.asarray(inp["c_ctx"], np.float32)
    in_maps = []
    for core in range(n_cores):
        b0 = core * NSEQ
        m = dict(shared)
        m["x"] = np.ascontiguousarray(x[b0:b0 + NSEQ, :S_len])
        m["ctx"] = np.ascontiguousarray(ctx[b0:b0 + NSEQ])
        cvecs = [c[b0 + i] for i in range(NSEQ)] + [c_ctx]
        m["small"] = small_pack(inp, L, cvecs)
        in_maps.append(m)
    res = run_bass_kernel_spmd(nc, in_maps, core_ids=list(range(n_cores)), **({"trace": True} if trace else {}))
    out = np.concatenate([np.asarray(r["out"], np.float32) for r in res.results], axis=0)
    return out, res


def kernel(**inputs):
    out, _ = run(inputs, 4096, 2, 4, 8)
    return out
```

```python
import numpy as np
import ml_dtypes
import concourse.bass as bass
import concourse.mybir as mybir
from concourse.bass_utils import run_bass_kernel_spmd

F32 = mybir.dt.float32
BF16 = mybir.dt.bfloat16
ALU = mybir.AluOpType
AF = mybir.ActivationFunctionType

D = 1024
CTX = 256
DFF = 2816
NPAIR = 22
EPS = 1e-6
ENGS = ("pe", "act", "dve", "pool", "sp")
DEBUG = False
B_SCALE = 1.0
ATT_DELAY = 60.0
CVT_COST = 50.0


class Sched:
    def __init__(self, nc, n_dma_sems=(("sp", 24), ("pool", 12), ("act", 4))):
        self.nc = nc
        self.q = {e: [] for e in ENGS}
        self.cnt = {e: 0 for e in ENGS}
        self.seen = {e: {} for e in ENGS}
        self.esem = {}
        self.dsem = {}
        self.dsem_rr = {}
        self.dsem_val = {}
        self._ctx = []
        for e in ENGS:
            cm = nc.semaphore("s_" + e)
            self.esem[e] = cm.__enter__()
            self._ctx.append(cm)
        for e, n in n_dma_sems:
            lst = []
            for i in range(n):
                cm = nc.semaphore("d_%s_%d" % (e, i))
                s = cm.__enter__()
                self._ctx.append(cm)
                lst.append(s)
                self.dsem_val[id(s)] = 0
            self.dsem[e] = lst
            self.dsem_rr[e] = 0
        self.lastw = {}
        self.readers = {}
        self.n_wait = 0
        self.n_ins = 0

    def close(self):
        for cm in reversed(self._ctx):
            cm.__exit__(None, None, None)

    def _deps(self, eng, reads, writes):
        waits = {}

        def add(tok):
            sem, val, src = tok
            if src == "pe" and eng == "pe":
                return
            key = id(sem)
            if self.seen[eng].get(key, 0) >= val:
                return
            if key not in waits or waits[key][1] < val:
                waits[key] = (sem, val)

        for k in reads:
            t = self.lastw.get(k)
            if t is not None:
                add(t)
        for k in writes:
            t = self.lastw.get(k)
            if t is not None:
                add(t)
            for t in self.readers.get(k, {}).values():
                add(t)
        for key, (sem, val) in waits.items():
            self.seen[eng][key] = val
        return list(waits.values())

    def _commit(self, tok, reads, writes):
        for k in reads:
            r = self.readers.setdefault(k, {})
            r[(id(tok[0]), tok[2] == "pe")] = tok
        for k in writes:
            self.lastw[k] = tok
            self.readers[k] = {}

    def op(self, eng, fn, reads=(), writes=()):
        waits = self._deps(eng, reads, writes)
        self.cnt[eng] += 1
        tok = (self.esem[eng], self.cnt[eng], eng)
        self.q[eng].append((fn, waits, (self.esem[eng], 1)))
        self._commit(tok, reads, writes)
        self.n_ins += 1
        self.n_wait += len(waits)
        return tok

    def dma(self, eng, fn, reads=(), writes=()):
        lst = self.dsem[eng]
        s = lst[self.dsem_rr[eng] % len(lst)]
        self.dsem_rr[eng] += 1
        waits = self._deps(eng, reads, writes)
        prev = self.dsem_val[id(s)]
        if prev > 0 and self.seen[eng].get(id(s), 0) < prev:
            waits = [w for w in waits if w[0] is not s]
            waits.append((s, prev))
            self.seen[eng][id(s)] = prev
        self.dsem_val[id(s)] = prev + 16
        tok = (s, prev + 16, "dma")
        self.q[eng].append((fn, waits, (s, 16)))
        self._commit(tok, reads, writes)
        self.n_ins += 1
        self.n_wait += len(waits)
        return tok

    def handoff(self, old_keys, new_keys):
        toks = {}
        for k in old_keys:
            cands = []
            t = self.lastw.get(k)
            if t is not None:
                cands.append(t)
            cands.extend(self.readers.get(k, {}).values())
            for t in cands:
                kk = (id(t[0]), t[2] == "pe")
                if kk not in toks or toks[kk][1] < t[1]:
                    toks[kk] = t
        for k in new_keys:
            self.lastw[k] = None
            self.readers[k] = dict(toks)

    def wait_all(self, eng):
        waits = []
        for e in ENGS:
            if self.cnt[e] > 0 and e != eng:
                waits.append((self.esem[e], self.cnt[e]))
        for e, lst in self.dsem.items():
            for s in lst:
                v = self.dsem_val[id(s)]
                if v > 0:
                    waits.append((s, v))
        self.q[eng].append((None, waits, None))

    def emit(self):
        nc = self.nc
        q = self.q

        def run(engobj, items):
            for fn, waits, inc in items:
                if fn is None:
                    for sem, val in waits:
                        engobj.wait_ge(sem, val)
                    continue
                for sem, val in waits[:-1]:
                    engobj.wait_ge(sem, val)
                ins = fn(engobj)
                if waits:
                    sem, val = waits[-1]
                    ins._wait_ge(sem, val)
                ins.then_inc(inc[0], inc[1])

        with nc.Block() as block:
            @block.sync
            def _(e):
                run(e, q["sp"])

            @block.tensor
            def _(e):
                run(e, q["pe"])

            @block.scalar
            def _(e):
                run(e, q["act"])

            @block.vector
            def _(e):
                run(e, q["dve"])

            @block.gpsimd
            def _(e):
                run(e, q["pool"])


def small_layout(L):
    off = {}
    o = 0
    for name, n in (("bmod", L * 48), ("n1g", L * 8), ("n2g", L * 8), ("fng", 8), ("cw", L * 16),
                    ("cb", L * 4), ("bax", L * 16), ("lam", L * 8), ("qg", L * 2), ("sink", L * 4),
                    ("fw", L * 44 * 3), ("fb", L * 44), ("cvec", 32)):
        off[name] = o
        o += n
    return off, o


W_IN_PERM = None


def w_in_perm():
    r = list(range(0, 512))
    gate = list(range(512, 1024))
    gq = lambda h: list(range(1024 + 64 * h, 1024 + 64 * h + 64))
    gk = list(range(1280, 1408))
    gv = list(range(1408, 1536))
    wq = lambda h: list(range(1536 + 64 * h, 1536 + 64 * h + 64))
    wk = list(range(1792, 1920))
    wv = list(range(1920, 2048))
    return np.array(r + gate + gq(0) + gq(2) + gq(1) + gq(3) + gk + wq(0) + wq(2) + wq(1) + wq(3) + wk + gv + wv)


def build(S_len, NSEQ, L, wplan=None):
    T = CTX + S_len
    NKB = T // 128
    NB = S_len // 128
    WB = S_len + 262
    WU = S_len + 259
    nc = bass.Bass("TRN2", target_bir_lowering=False)
    S = Sched(nc)
    SO, NS = small_layout(L)

    def din(name, shape, dt=F32):
        return nc.dram_tensor(name, list(shape), dt, kind="ExternalInput").ap()

    def dscr(name, shape, dt):
        return nc.dram_tensor(name, list(shape), dt, kind=("ExternalOutput" if DEBUG else "Internal")).ap()

    x_d = din("x", [NSEQ, S_len, D])
    ctx_d = din("ctx", [NSEQ, CTX, D])
    out_d = nc.dram_tensor("out", [NSEQ, S_len, D], F32, kind="ExternalOutput").ap()
    small_d = din("small", [128, NS])
    cf32_d = din("cf32", [128, 256])
    cb16_d = din("cb16", [128, 768], BF16)
    rope_d = din("rope", [2, 128, S_len])
    wmod_d = din("w_mod", [L, D, 6 * D])
    win_d = [din("w_in%d" % l, [1024, 2048]) for l in range(L)]
    wout_d = [din("w_out%d" % l, [512, 2048]) for l in range(L)]
    wup_d = [din("w_up%d" % l, [NPAIR * 128, 2048]) for l in range(L)]
    wdn_d = [din("w_dn%d" % l, [8 * 128, NPAIR * 128]) for l in range(L)]
    wbd_d = [din("w_bd%d" % l, [128, 16 * 128]) for l in range(L)]
    win_b = [dscr("w_in_b%d" % l, [1024, 2048], BF16) for l in range(L)]
    wout_b = [dscr("w_out_b%d" % l, [512, 2048], BF16) for l in range(L)]
    wup_b = [dscr("w_up_b%d" % l, [NPAIR * 128, 2048], BF16) for l in range(L)]
    wdn_b = [dscr("w_dn_b%d" % l, [8 * 128, NPAIR * 128], BF16) for l in range(L)]
    hTT = [[dscr("hT%d_%d" % (par, b), [D, T], F32) for b in range(NSEQ)] for par in range(2)]
    RT = [dscr("RT%d" % b, [512, T], F32) for b in range(NSEQ)]
    GT = [dscr("GT%d" % b, [512, T], F32) for b in range(NSEQ)]
    QgT = [dscr("QgT%d" % b, [2, 128, T], BF16) for b in range(NSEQ)]
    QwT = [dscr("QwT%d" % b, [2, 128, T], BF16) for b in range(NSEQ)]
    KgT = [dscr("KgT%d" % b, [128, T], BF16) for b in range(NSEQ)]
    KwT = [dscr("KwT%d" % b, [128, T], BF16) for b in range(NSEQ)]
    VV = [dscr("VV%d" % b, [T, 256], BF16) for b in range(NSEQ)]
    mixT = [dscr("mixT%d" % b, [D, T], BF16) for b in range(NSEQ)]
    a2T = [dscr("a2T%d" % b, [D, T], BF16) for b in range(NSEQ)]

    class Arena:
        def __init__(self, base):
            self.o = base
            self.hi = base

        def a(self, name, shape, dt):
            nbytes = int(np.prod(shape[1:])) * (4 if dt == F32 else 2)
            self.o = (self.o + 63) // 64 * 64
            t = nc.alloc_sbuf_tensor_at(name, list(shape), dt, offset=self.o)
            self.o += nbytes
            self.hi = max(self.hi, self.o)
            return t

    ar = Arena(16512)
    SM = ar.a("SM", [128, NS], F32)
    CF = ar.a("CF", [128, 256], F32)
    CB = ar.a("CB", [128, 768], BF16)
    DS_SC = ar.a("DS_SC", [128, 8, 4], F32)
    MODV = ar.a("MODV", [128, L, 48, 4], F32)
    A1 = ar.a("A1", [128, L, 8, 4], F32)
    A2 = ar.a("A2", [128, L, 8, 4], F32)
    NSP = ar.a("NSP", [128, L, 8], F32)
    ESK = ar.a("ESK", [128, L, 4], F32)
    NBAX = ar.a("NBAX", [128, L * 16], F32)
    NWU = 7
    WUr = [ar.a("WU%d" % i, [128, 2048], BF16) for i in range(NWU)]
    NWD = 3
    WDr = [ar.a("WD%d" % i, [128, NPAIR, 128], BF16) for i in range(NWD)]
    WBD = ar.a("WBD", [128, 16, 128], BF16)
    big0 = ar.o
    va = Arena(big0)
    Xb = [va.a("X%d" % i, [128, 8, 512], F32) for i in range(2)]
    XS = va.a("XS", [128, 8, 512], BF16)
    Ab = [va.a("A%d" % i, [128, 8, 512], BF16) for i in range(2)]
    MXb = [va.a("MX%d" % i, [128, 8, 512], BF16) for i in range(2)]
    ACTT = va.a("ACTT", [128, NPAIR, 512], BF16)
    WINa = nc.alloc_sbuf_tensor_at("WINa", [128, 4, 2048], BF16, offset=va.o - 22 * 1024 - 16 * 1024)
    WINb = nc.alloc_sbuf_tensor_at("WINb", [128, 4, 2048], BF16, offset=va.o - 22 * 1024)
    NF = 12
    Fb = [va.a("F%d" % i, [128, 512], F32) for i in range(NF)]
    NH = 8
    Hb = [va.a("H%d" % i, [128, 512], BF16) for i in range(NH)]
    RSb = [va.a("RS%d" % i, [128, 512], F32) for i in range(2)]
    CTb = [va.a("CT%d" % i, [128, 512], F32) for i in range(2)]
    STb = [va.a("ST%d" % i, [128, 512], F32) for i in range(2)]
    XINb = [va.a("XIN%d" % i, [128, 1024], F32) for i in range(2)]
    keysAE = (["X0", "X1", "XS", "A0", "A1", "MX0", "MX1", "ACTT", "RS0", "RS1", "CT0", "CT1", "ST0", "ST1",
               "XIN0", "XIN1"] + ["F%d" % i for i in range(NF)] + ["H%d" % i for i in range(NH)])
    vb = Arena(big0)
    Bb = [vb.a("B%d" % i, [128, WB], F32) for i in range(5)]
    UB = vb.a("UB", [128, WB], BF16)
    keysB = ["B0", "B1", "B2", "B3", "B4", "UB"]
    vc = Arena(vb.hi)
    KTb = [vc.a("KT%d" % i, [128, T], BF16) for i in range(1)]
    VAb = [vc.a("VA%d" % i, [128, NKB, 2, 128], BF16) for i in range(1)]
    QTb = [vc.a("QT%d" % i, [128, 2, 512], BF16) for i in range(4)]
    NPT = 4
    PTb = [vc.a("PT%d" % i, [128, 512], BF16) for i in range(NPT)]
    RCb = [vc.a("RC%d" % i, [128, 512], F32) for i in range(2)]
    YBb = [vc.a("YB%d" % i, [128, 512], BF16) for i in range(2)]
    keysCD = (["KT0", "VA0", "QT0", "QT1", "QT2", "QT3", "RC0", "RC1", "YB0", "YB1"] + ["PT%d" % i for i in range(NPT)])
    hi = max(va.hi, vb.hi, vc.hi)
    assert hi <= 229376 - 64, hi
    PS = [nc.alloc_psum_tensor("P%d" % i, [128, 512], F32) for i in range(8)]

    def P(i):
        return ("P", i)

    cur_view = ["AE"]
    view_keys = {"AE": keysAE, "BCD": keysB + keysCD}

    def switch(view):
        if cur_view[0] == view:
            return
        S.handoff(view_keys[cur_view[0]], view_keys[view])
        cur_view[0] = view

    class Ring:
        def __init__(self, names, tiles):
            self.names = names
            self.tiles = tiles
            self.i = 0

        def next(self):
            j = self.i % len(self.tiles)
            self.i += 1
            return self.names[j], self.tiles[j]

    Fr = Ring(["F%d" % i for i in range(NF)], Fb)
    Hr = Ring(["H%d" % i for i in range(NH)], Hb)
    PTr = Ring(["PT%d" % i for i in range(NPT)], PTb)

    def sm(name, idx):
        o = SO[name] + idx
        return SM[:, o:o + 1]

    def sm_rng(name, idx, n):
        o = SO[name] + idx
        return SM[:, o:o + n]

    ident = CF[:, 0:128]
    perm = CF[:, 128:256]
    ones_s = CB[:, 0:128]
    blk_s = CB[:, 128:256]

    def mask(side):
        return CB[:, 256 + side * 256: 256 + side * 256 + 256]

    def mm(out, lhsT, rhs, start, stop, reads, writes):
        S.op("pe", lambda e: e.matmul(out, lhsT=lhsT, rhs=rhs, start=start, stop=stop), reads, writes)

    def act(out, in_, func, reads, writes, bias=None, scale=None, eng="act"):
        kw = {}
        if bias is not None:
            kw["bias"] = bias
        if scale is not None:
            kw["scale"] = scale
        S.op("act", lambda e: e.activation(out=out, in_=in_, func=func, **kw), reads, writes)

    def tt(eng, out, in0, in1, op, reads, writes):
        S.op(eng, lambda e: e.tensor_tensor(out=out, in0=in0, in1=in1, op=op), reads, writes)

    def ts(eng, out, in0, s1, s2, op0, op1, reads, writes):
        if op1 is None:
            S.op(eng, lambda e: e.tensor_scalar(out=out, in0=in0, scalar1=s1, scalar2=None, op0=op0), reads, writes)
        else:
            S.op(eng, lambda e: e.tensor_scalar(out=out, in0=in0, scalar1=s1, scalar2=s2, op0=op0, op1=op1), reads, writes)

    def stt(out, in0, scalar, in1, op0, op1, reads, writes):
        S.op("dve", lambda e: e.scalar_tensor_tensor(out=out, in0=in0, scalar=scalar, in1=in1, op0=op0, op1=op1), reads, writes)

    def recip(out, in_, reads, writes):
        S.op("dve", lambda e: e.reciprocal(out=out, in_=in_), reads, writes)

    def cp(eng, out, in_, reads, writes):
        if eng == "act":
            S.op("act", lambda e: e.activation(out=out, in_=in_, func=AF.Copy), reads, writes)
        else:
            S.op(eng, lambda e: e.tensor_copy(out=out, in_=in_), reads, writes)

    def mset(eng, ap, val, writes):
        S.op(eng, lambda e: e.memset(ap, val), (), writes)

    def load(out, in_, reads, writes, eng="sp"):
        S.dma(eng, lambda e: e.dma_start(out=out, in_=in_), reads, writes)

    def store(out, in_, reads, writes, eng="pool"):
        S.dma(eng, lambda e: e.dma_start(out=out, in_=in_), reads, writes)

    def tiles_AE():
        r = [(0, 0, CTX)]
        for j in range(S_len // 512):
            r.append((1 + j, CTX + 512 * j, 512))
        return r

    def ffn_tiles(need_ctx):
        r = []
        if need_ctx:
            r.append((0, 0, CTX, CTX))
        o0 = 0
        while o0 < S_len:
            n = min(456 if S_len % 456 else 510, S_len - o0)
            r.append((CTX, o0, n, S_len))
            o0 += n
        return r

    def wsrc(spec):
        name, l, idx = spec
        if name == "wout":
            return wout_b[l][idx * 128:(idx + 1) * 128, :]
        if name == "wup":
            return wup_b[l][idx * 128:(idx + 1) * 128, :]
        return wdn_b[l][idx * 128:(idx + 1) * 128, :]

    class WStream:
        def __init__(self):
            self.record = wplan is None
            self.items = [] if self.record else list(wplan)
            self.req = 0
            self.ring = {"wu": (WUr, NWU), "wd": (WDr, NWD)}
            self.cnt = {"wu": 0, "wd": 0}
            self.fifo = {"wu": [], "wd": []}
            self.bykind = {k: [sp for (kk, sp) in self.items if kk == k] for k in ("wu", "wd")}
            self.issued = {"wu": 0, "wd": 0}

        def _issue(self, kind, spec):
            src = wsrc(spec)
            tiles, n = self.ring[kind]
            j = self.cnt[kind] % n
            self.cnt[kind] += 1
            key = "%s%d" % (kind.upper(), j)
            if kind == "wu":
                load(tiles[j][:], src, [("wcv", src.tensor.name)], [key])
            else:
                load(tiles[j][:].rearrange("p i m -> p (i m)"), src, [("wcv", src.tensor.name)], [key])
            self.fifo[kind].append((key, tiles[j]))
            self.issued[kind] += 1

        def _topup(self):
            for k in ("wu", "wd"):
                lst = self.bykind[k]
                _, n = self.ring[k]
                while self.issued[k] < len(lst) and len(self.fifo[k]) < n - 1:
                    self._issue(k, lst[self.issued[k]])

        def get(self, kind, spec):
            if self.record:
                self.items.append((kind, spec))
                self._issue(kind, spec)
                return self.fifo[kind].pop(0)
            assert self.items[self.req] == (kind, spec), (self.req, self.items[self.req], kind, spec)
            self.req += 1
            if not self.fifo[kind]:
                self._topup()
            r = self.fifo[kind].pop(0)
            self._topup()
            return r

    load(SM[:], small_d, [], ["SM"])
    load(CF[:], cf32_d, [], ["CF"])
    load(CB[:], cb16_d, [], ["CB"])

    def convert_weights(l):
        for src, dst in ((win_d[l], win_b[l]), (wout_d[l], wout_b[l]), (wup_d[l], wup_b[l]), (wdn_d[l], wdn_b[l])):
            R = src.shape[0]
            W = src.shape[1]
            step = 1024
            for r0 in range(0, R, step):
                r1 = min(R, r0 + step)
                if W <= 2048:
                    store(dst[r0:r1, :], src[r0:r1, :], [], [("wcv", dst.tensor.name)])
                else:
                    for c0 in range(0, W, 1408):
                        store(dst[r0:r1, c0:c0 + 1408], src[r0:r1, c0:c0 + 1408], [], [("wcv", dst.tensor.name)])

    convert_weights(0)

    def convert_pieces(l):
        for src, dst in ((win_d[l], win_b[l]), (wout_d[l], wout_b[l]), (wup_d[l], wup_b[l]), (wdn_d[l], wdn_b[l])):
            R = src.shape[0]
            W = src.shape[1]
            for r0 in range(0, R, 256):
                r1 = min(R, r0 + 256)
                if W <= 2048:
                    store(dst[r0:r1, :], src[r0:r1, :], [], [("wcv", dst.tensor.name)])
                else:
                    for c0 in range(0, W, 1408):
                        store(dst[r0:r1, c0:c0 + 1408], src[r0:r1, c0:c0 + 1408], [], [("wcv", dst.tensor.name)])
                yield CVT_COST

    act(DS_SC[:].rearrange("p k v -> p (k v)"), sm_rng("cvec", 0, 32), AF.Silu, ["SM"], ["SC"])
    for l in range(L):
        act(NSP[:, l, :], sm_rng("lam", l * 8, 8), AF.Exp, ["SM"], ["NSP"], scale=-1.0)
        act(NSP[:, l, :], NSP[:, l, :], AF.Ln, ["NSP"], ["NSP"], bias=1.0)
        ts("dve", NSP[:, l, :], NSP[:, l, :], -8.0, None, ALU.mult, None, ["NSP"], ["NSP"])
        act(ESK[:, l, :], sm_rng("sink", l * 4, 4), AF.Exp, ["SM"], ["ESK"])

    ts("dve", NBAX[:], sm_rng("bax", 0, L * 16), -1.0, None, ALU.mult, None, ["SM"], ["NBAX"])
    mod_ps = PS[2]
    SCB = Hb[0][:, 0:32].rearrange("p (k v) -> p k v", v=4)
    cp("act", Hb[0][:, 0:32], DS_SC[:].rearrange("p k v -> p (k v)"), ["SC"], ["H0"])
    wm_tiles = [("A0", Ab[0]), ("A1", Ab[1]), ("MX0", MXb[0]), ("MX1", MXb[1])]
    for l in range(L):
        for jg in range(12):
            nm, xt = wm_tiles[jg % 4]
            store(xt[:], wmod_d[l].rearrange("(k p) j -> p k j", p=128)[:, :, jg * 512:(jg + 1) * 512], [], [nm])
            for jj in range(4):
                c = jg * 4 + jj
                for k in range(8):
                    mm(mod_ps[:, c * 4:c * 4 + 4], xt[:, k, jj * 128:(jj + 1) * 128], SCB[:, k, :],
                       k == 0, k == 7, [nm, "H0"], [P(2)])
        for v in range(3):
            tt("dve", MODV[:, l, :, v], mod_ps[:, 0:192].rearrange("p (c v) -> p c v", v=4)[:, :, v],
               sm_rng("bmod", l * 48, 48), ALU.add, [P(2), "SM"], ["MODV"])
        for v in range(3):
            stt(A1[:, l, :, v], MODV[:, l, 8:16, v], 1.0, sm_rng("n1g", l * 8, 8), ALU.add, ALU.mult, ["MODV", "SM"], ["A1"])
            stt(A2[:, l, :, v], MODV[:, l, 32:40, v], 1.0, sm_rng("n2g", l * 8, 8), ALU.add, ALU.mult, ["MODV", "SM"], ["A2"])

    def modsc(l, which, k, v):
        return MODV[:, l, which * 8 + k, v:v + 1]


    def stage0(b):
        xi = 0
        for (ti, t0, n) in tiles_AE():
            nm, xt = "X%d" % (ti % 2), Xb[ti % 2]
            for tb in range(n // 128):
                src = ctx_d[b, tb * 128:(tb + 1) * 128, :] if ti == 0 else x_d[b, (t0 - CTX) + tb * 128:(t0 - CTX) + (tb + 1) * 128, :]
                xn, xin = "XIN%d" % (xi % 2), XINb[xi % 2]
                xi += 1
                load(xin[:], src, [], [xn])
                for half in range(2):
                    pb = half
                    for kk in range(4):
                        k = half * 4 + kk
                        S.op("pe", lambda e, o=PS[pb][:, kk * 128:(kk + 1) * 128], i=xin[:, k * 128:(k + 1) * 128]:
                             e.transpose(out=o, in_=i, identity=ident), [xn, "CF"], [P(pb)])
                    dst = xt[:, half * 4:half * 4 + 4, tb * 128:(tb + 1) * 128]
                    cp("act" if half == 0 else "dve", dst, PS[pb][:, :].rearrange("p (k t) -> p k t", t=128), [P(pb)], [nm])
            store(hTT[0][b].rearrange("(k p) t -> p k t", p=128)[:, :, t0:t0 + n], xt[:, :, 0:n], [nm], [("hT", 0, b, ti)])

    def norm_steps(xn, xt, an, at, n, Acoef, Bcoef_which, l, v, st_bank=0):
        rn, rs = "RS0", RSb[0]
        steps = []

        def s_sq():
            act(XS[:, :, 0:n], xt[:, :, 0:n], AF.Square, [xn], ["XS"])

        def s_mm():
            for k in range(8):
                mm(PS[st_bank][:, 0:n], ones_s, XS[:, k, 0:n], k == 0, k == 7, ["XS", "CB"], [P(st_bank)])

        def s_rs():
            act(rs[:, 0:n], PS[st_bank][:, 0:n], AF.Ln, [P(st_bank)], [rn], bias=EPS)
            act(rs[:, 0:n], rs[:, 0:n], AF.Exp, [rn], [rn], scale=-0.5)

        def s_k(k):
            fn, ft = Fr.next()
            stt(ft[:, 0:n], xt[:, k, 0:n], Acoef[:, l, k, v:v + 1], rs[:, 0:n], ALU.mult, ALU.mult, [xn, rn, "A1", "A2"], [fn])
            act(at[:, k, 0:n], ft[:, 0:n], AF.Identity, [fn, "MODV"], [an], bias=modsc(l, Bcoef_which, k, v))

        steps += [s_sq, s_mm, s_rs]
        for k in range(8):
            steps.append(lambda k=k: s_k(k))
        return steps

    def run_steps(steps, k=None):
        m = len(steps) if k is None else min(k, len(steps))
        for _ in range(m):
            steps.pop(0)()

    qpend = []

    def advance_q(flush=False):
        while True:
            for st in list(qpend):
                st.pop(0)()
                if not st:
                    qpend.remove(st)
            if not flush or not qpend:
                break

    def qk_post(pb, n, is_lat, gain, dst, tl0, ctn, ctt, stn, stt_, wkeys):
        hn, ht = Hr.next()
        st = {}
        steps = []
        if gain is not None:
            sn, sq = Hr.next()
            rawn, raw = Fr.next()
            cp("act", raw[:, 0:n], PS[pb][:, 0:n], [P(pb)], [rawn])
            act(sq[:, 0:n], raw[:, 0:n], AF.Square, [rawn], [sn])
            rn, rs = "RS1", RSb[1]

            def s2():
                mm(PS[1][:, 0:n], blk_s, sq[:, 0:n], True, True, [sn, "CB"], [P(1)])
                act(rs[:, 0:n], PS[1][:, 0:n], AF.Ln, [P(1)], [rn], bias=EPS)
                act(rs[:, 0:n], rs[:, 0:n], AF.Exp, [rn], [rn], scale=-0.5)
                if is_lat:
                    st["f"] = Fr.next()
                    stt(st["f"][1][:, 0:n], raw[:, 0:n], gain, rs[:, 0:n], ALU.mult, ALU.mult, [rawn, rn, "SM"], [st["f"][0]])
                else:
                    stt(ht[:, 0:n], raw[:, 0:n], gain, rs[:, 0:n], ALU.mult, ALU.mult, [rawn, rn, "SM"], [hn])
                    store(dst, ht[:, 0:n], [hn], wkeys)

            steps.append(s2)
        else:
            if is_lat:
                st["f"] = Fr.next()
                cp("act", st["f"][1][:, 0:n], PS[pb][:, 0:n], [P(pb)], [st["f"][0]])
            else:
                cp("act", ht[:, 0:n], PS[pb][:, 0:n], [P(pb)], [hn])
                store(dst, ht[:, 0:n], [hn], wkeys)
        if is_lat:
            def s3():
                fn, ft = st["f"]
                mm(PS[2][:, 0:n], perm, ft[:, 0:n], True, True, [fn, "CF"], [P(2)])
                f2n, f2 = Fr.next()
                tt("dve", f2[:, 0:n], PS[2][:, 0:n], stt_[:, 0:n], ALU.mult, [P(2), stn], [f2n])
                f3n, f3 = Fr.next()
                tt("pool", f3[:, 0:n], ft[:, 0:n], ctt[:, 0:n], ALU.mult, [fn, ctn], [f3n])
                tt("dve", ht[:, 0:n], f2[:, 0:n], f3[:, 0:n], ALU.add, [f2n, f3n], [hn])
                store(dst, ht[:, 0:n], [hn], wkeys)

            steps.append(s3)
        if steps:
            qpend.append(steps)

    def load_win(l, half):
        src = win_b[l].rearrange("(u p) f -> p u f", p=128)[:, half * 4:half * 4 + 4, :]
        if half == 0:
            load(WINa[:], src, [("wcv", win_b[l].tensor.name)], ["MX0", "MX1"])
        else:
            load(WINb[:], src, [("wcv", win_b[l].tensor.name)], ["ACTT"])

    def stageA(l, b, ws, first):
        tl = tiles_AE()
        if b == 0:
            if first:
                load_win(l, 0)
            load_win(l, 1)

        def prep(ti, t0, n):
            is_lat = ti > 0
            v = b if is_lat else NSEQ
            xn, xt = "X%d" % (ti % 2), Xb[ti % 2]
            an, at = "A%d" % (ti % 2), Ab[ti % 2]
            ctn, ctt = "CT%d" % (ti % 2), CTb[ti % 2]
            stn, stt_ = "ST%d" % (ti % 2), STb[ti % 2]

            def s_load():
                load(xt[:, :, 0:n], hTT[l % 2][b].rearrange("(k p) t -> p k t", p=128)[:, :, t0:t0 + n], [("hT", l % 2, b, ti)], [xn])
                if is_lat:
                    load(ctt[:, 0:n], rope_d[0, :, t0 - CTX:t0 - CTX + n], [], [ctn])
                    load(stt_[:, 0:n], rope_d[1, :, t0 - CTX:t0 - CTX + n], [], [stn])

            return [s_load] + norm_steps(xn, xt, an, at, n, A1, 0, l, v)

        nxt = prep(*tl[0])
        for idx, (ti, t0, n) in enumerate(tl):
            run_steps(nxt)
            nxt = prep(*tl[idx + 1]) if idx + 1 < len(tl) else []
            is_lat = ti > 0
            v = b if is_lat else NSEQ
            xn, xt = "X%d" % (ti % 2), Xb[ti % 2]
            an, at = "A%d" % (ti % 2), Ab[ti % 2]
            ctn, ctt = "CT%d" % (ti % 2), CTb[ti % 2]
            stn, stt_ = "ST%d" % (ti % 2), STb[ti % 2]
            pj = 0
            for u in range(8):
                if u >= 1:
                    run_steps(nxt, 2)
                if u < 4:
                    wk, w3 = ["MX0", "MX1"], WINa[:, u, :].rearrange("p (k c) -> p k c", k=8)
                else:
                    wk, w3 = ["ACTT"], WINb[:, u - 4, :].rearrange("p (k c) -> p k c", k=8)
                if u < 7:
                    for half in range(2):
                        ci = 2 * u + half
                        pb = 3 + (pj % 4)
                        pj += 1
                        for k in range(8):
                            mm(PS[pb][:, 0:n], w3[:, k, half * 128:(half + 1) * 128], at[:, k, 0:n], k == 0, k == 7, wk + [an], [P(pb)])
                        advance_q()
                        if ci < 4:
                            fn, ft = Fr.next()
                            cp("act", ft[:, 0:n], PS[pb][:, 0:n], [P(pb)], [fn])
                            store(RT[b][ci * 128:(ci + 1) * 128, t0:t0 + n], ft[:, 0:n], [fn], [("RT", b, ci)])
                        elif ci < 8:
                            fn, ft = Fr.next()
                            act(ft[:, 0:n], PS[pb][:, 0:n], AF.Gelu_apprx_tanh, [P(pb)], [fn])
                            store(GT[b][(ci - 4) * 128:(ci - 3) * 128, t0:t0 + n], ft[:, 0:n], [fn], [("GT", b, ci - 4)])
                        elif ci in (8, 9):
                            qk_post(pb, n, is_lat, sm("qg", l * 2 + 0), QgT[b][ci - 8, :, t0:t0 + n], t0 - CTX, ctn, ctt, stn, stt_, [("QgT", b)])
                        elif ci == 10:
                            qk_post(pb, n, is_lat, sm("qg", l * 2 + 1), KgT[b][:, t0:t0 + n], t0 - CTX, ctn, ctt, stn, stt_, [("KgT", b)])
                        elif ci in (11, 12):
                            qk_post(pb, n, is_lat, None, QwT[b][ci - 11, :, t0:t0 + n], t0 - CTX, ctn, ctt, stn, stt_, [("QwT", b)])
                        else:
                            qk_post(pb, n, is_lat, None, KwT[b][:, t0:t0 + n], t0 - CTX, ctn, ctt, stn, stt_, [("KwT", b)])
                else:
                    for tb in range(n // 128):
                        pb = 7
                        for k in range(8):
                            mm(PS[pb][:, 0:256], at[:, k, tb * 128:(tb + 1) * 128], w3[:, k, :], k == 0, k == 7, wk + [an], [P(pb)])
                        hn, ht = Hr.next()
                        cp("act" if tb % 2 else "dve", ht[:, 0:256], PS[pb][:, 0:256], [P(pb)], [hn])
                        store(VV[b][t0 + tb * 128:t0 + (tb + 1) * 128, :], ht[:, 0:256], [hn], [("VV", b)])
                        if tb % 2 == 1:
                            advance_q()
        advance_q(flush=True)

    def rev_ap(t, c0, n):
        a = t[:, c0:c0 + n]
        return bass.AP(a.tensor, a.offset + n - 1, [list(a.ap[0]), [-1, n]])

    def stageB(l, b):
        Rp, U, A_, D_, Hf = Bb
        for c in range(4):
            mset("pool", Rp[:, 0:2], 0.0, ["B0"])
            mset("pool", Rp[:, 258:261], 0.0, ["B0"])
            mset("pool", Rp[:, 261 + S_len:262 + S_len], 0.0, ["B0"])
            load(Rp[:, 2:258], RT[b][c * 128:(c + 1) * 128, 0:CTX], [("RT", b, c)], ["B0"])
            load(Rp[:, 261:261 + S_len], RT[b][c * 128:(c + 1) * 128, CTX:T], [("RT", b, c)], ["B0"])
            cwo = l * 16 + c * 4
            ts("dve", U[:, 0:WU], Rp[:, 0:WU], sm("cw", cwo), sm("cb", l * 4 + c), ALU.mult, ALU.add, ["B0", "SM"], ["B1"])
            for k in range(1, 4):
                stt(U[:, 0:WU], Rp[:, k:k + WU], sm("cw", cwo + k), U[:, 0:WU], ALU.mult, ALU.add, ["B0", "B1", "SM"], ["B1"])
            yield 18.0
            cp("pool", UB[:, 0:WU], U[:, 0:WU], ["B1"], ["UB"])
            yield 6.0
            for d in range(2):
                pbi = 0
                for j0 in range(0, WU, 512):
                    n = min(512, WU - j0)
                    for gate, dstt, dn in ((0, A_, "B2"), (1, D_, "B3")):
                        pb = 5 + pbi % 3
                        pbi += 1
                        mm(PS[pb][:, 0:n], WBD[:, (gate * 2 + d) * 4 + c, :], UB[:, j0:j0 + n], True, True, ["UB", "WBD"], [P(pb)])
                        act(dstt[:, j0:j0 + n], PS[pb][:, 0:n], AF.Sigmoid, [P(pb), "SM"], [dn],
                            bias=sm("bax", l * 16 + (gate * 2 + d) * 4 + c))
                yield 22.0
                act(A_[:, 0:WU], A_[:, 0:WU], AF.Exp, ["B2", "NSP"], ["B2"], scale=NSP[:, l, d * 4 + c:d * 4 + c + 1])
                yield 3.0
                tt("dve", D_[:, 0:WU], D_[:, 0:WU], U[:, 0:WU], ALU.mult, ["B3", "B1"], ["B3"])
                tt("pool", Rp[:, 0:WU], A_[:, 0:WU], A_[:, 0:WU], ALU.mult, ["B2"], ["B0"])
                yield 4.5
                act(Rp[:, 0:WU], Rp[:, 0:WU], AF.Ln, ["B0"], ["B0"], scale=-1.0, bias=1.0)
                act(Rp[:, 0:WU], Rp[:, 0:WU], AF.Exp, ["B0"], ["B0"], scale=0.5)
                yield 6.0
                tt("dve", D_[:, 0:WU], D_[:, 0:WU], Rp[:, 0:WU], ALU.mult, ["B3", "B0"], ["B3"])
                yield 4.5
                if d == 0:
                    mset("dve", A_[:, 256:259], 1.0, ["B2"])
                    mset("dve", D_[:, 256:259], 0.0, ["B3"])
                    S.op("dve", lambda e: e.tensor_tensor_scan(out=Hf[:, 0:WU], data0=A_[:, 0:WU], data1=D_[:, 0:WU], initial=0.0,
                                                               op0=ALU.mult, op1=ALU.add), ["B2", "B3"], ["B4"])
                else:
                    S.op("dve", lambda e: e.tensor_tensor_scan(out=rev_ap(Rp, 0, CTX), data0=rev_ap(A_, 0, CTX), data1=rev_ap(D_, 0, CTX),
                                                               initial=0.0, op0=ALU.mult, op1=ALU.add), ["B2", "B3"], ["B0"])
                    S.op("dve", lambda e: e.tensor_tensor_scan(out=rev_ap(Rp, 259, S_len), data0=rev_ap(A_, 259, S_len),
                                                               data1=rev_ap(D_, 259, S_len), initial=Rp[:, 0:1],
                                                               op0=ALU.mult, op1=ALU.add), ["B2", "B3", "B0"], ["B0"])
                yield 9.0
            load(A_[:, 0:CTX], GT[b][c * 128:(c + 1) * 128, 0:CTX], [("GT", b, c)], ["B2"])
            load(A_[:, 259:259 + S_len], GT[b][c * 128:(c + 1) * 128, CTX:T], [("GT", b, c)], ["B2"])
            tt("dve", Hf[:, 0:CTX], Hf[:, 0:CTX], Rp[:, 0:CTX], ALU.add, ["B4", "B0"], ["B4"])
            tt("dve", Hf[:, 259:WU], Hf[:, 259:WU], Rp[:, 259:WU], ALU.add, ["B4", "B0"], ["B4"])
            tt("pool", UB[:, 0:CTX], Hf[:, 0:CTX], A_[:, 0:CTX], ALU.mult, ["B4", "B2"], ["UB"])
            tt("pool", UB[:, 259:WU], Hf[:, 259:WU], A_[:, 259:WU], ALU.mult, ["B4", "B2"], ["UB"])
            store(mixT[b][c * 128:(c + 1) * 128, 0:CTX], UB[:, 0:CTX], ["UB"], [("mixT", b)])
            store(mixT[b][c * 128:(c + 1) * 128, CTX:T], UB[:, 259:WU], ["UB"], [("mixT", b)])
            yield 12.0

    def load_kv(b, which, Ksrc, vcol0, kkey, slot):
        kn, kt = "KT%d" % slot, KTb[slot]
        vn, vt = "VA%d" % slot, VAb[slot]
        load(kt[:], Ksrc[b][:, :], [(kkey, b)], [kn])
        for g in range(2):
            load(vt[:, :, g, 0:64], VV[b].rearrange("(kb p) c -> p kb c", p=128)[:, :, vcol0 + g * 64:vcol0 + g * 64 + 64],
                 [("VV", b)], [vn])
        return kn, kt, vn, vt

    def attention(l, b, need_ctx, window):
        slot = 0
        mset("pool", VAb[slot][:, :, :, 64:128], 1.0, ["VA%d" % slot])
        for j in range(2):
            mset("pool", QTb[2 * j + 0][64:128, :, :], 0.0, ["QT%d" % (2 * j)])
            mset("pool", QTb[2 * j + 1][0:64, :, :], 0.0, ["QT%d" % (2 * j + 1)])
        if window:
            kn, kt, vn, vt = load_kv(b, 1, KwT, 128, "KwT", slot)
            Qsrc, qkey, row0 = QwT, "QwT", 768
        else:
            kn, kt, vn, vt = load_kv(b, 0, KgT, 0, "KgT", slot)
            Qsrc, qkey, row0 = QgT, "QgT", 512
        groups = []
        if window:
            if need_ctx:
                for qb in range(2):
                    for g in range(2):
                        groups.append((qb * 128, 128, g, [(0, None), (1, None)]))
            for qb in range(NB):
                kbs = [(0, None), (1, None)]
                if qb > 0:
                    kbs.append((2 + qb - 1, 0))
                kbs.append((2 + qb, None))
                if qb < NB - 1:
                    kbs.append((2 + qb + 1, 1))
                for g in range(2):
                    groups.append((CTX + qb * 128, 128, g, kbs))
        else:
            if need_ctx:
                for g in range(2):
                    groups.append((0, 256, g, [(0, None), (1, None)]))
            for qt in range(S_len // 256):
                for g in range(2):
                    groups.append((CTX + qt * 256, 256, g, [(kb, None) for kb in range(NKB)]))
        steps = []
        for gi, (q0, nq, g, kbs) in enumerate(groups):
            for si, (kb, mk) in enumerate(kbs):
                steps.append((gi, si, kb, mk, si == 0, si == len(kbs) - 1))
        qt_loaded = {}
        qstate = {"i": 0, "cur": None}

        def q_tile(q0, nq):
            base = (q0 // 512) * 512 if q0 >= CTX else 0
            if q0 >= CTX:
                base = CTX + ((q0 - CTX) // 512) * 512
                width = min(512, T - base)
            else:
                base, width = 0, CTX
            if qstate["cur"] is None or qstate["cur"][0] != base:
                j = qstate["i"] % 2
                qstate["i"] += 1
                for gg in range(2):
                    load(QTb[2 * j + gg][64 * gg:64 * gg + 64, :, 0:width],
                         Qsrc[b].rearrange("x p t -> p x t")[64 * gg:64 * gg + 64, :, base:base + width], [(qkey, b)], ["QT%d" % (2 * j + gg)])
                qstate["cur"] = (base, j)
            base, j = qstate["cur"]
            return j, q0 - base

        sc_banks = [0, 1, 2]
        oc_banks = [3, 4]
        sc_i = [0]
        LA = 2
        pend = []
        scale = 0.125

        def issue_S(st):
            gi, si, kb, mk, first, last = st
            q0, nq, g, kbs = groups[gi]
            j, qo = q_tile(q0, nq)
            qn, qv = "QT%d" % (2 * j + g), QTb[2 * j + g][:, :, qo:qo + nq]
            pb = sc_banks[sc_i[0] % 3]
            sc_i[0] += 1
            N = 2 * nq
            mm(PS[pb][:, 0:N], kt[:, kb * 128:(kb + 1) * 128], qv, True, True, [kn, qn], [P(pb)])
            pend.append((st, pb, N))

        def finish(stp):
            st, pb, N = stp
            gi, si, kb, mk, first, last = st
            q0, nq, g, kbs = groups[gi]
            ob = oc_banks[gi % 2]
            pn, pt = PTr.next()
            act(pt[:, 0:N], PS[pb][:, 0:N], AF.Exp, [P(pb)], [pn], scale=scale)
            if mk is not None:
                tt("dve", pt[:, 0:N], pt[:, 0:N], mask(mk), ALU.mult, [pn, "CB"], [pn])
            mm(PS[ob][:, 0:N], vt[:, kb, g, :], pt[:, 0:N], first, last, [vn, pn], [P(ob)])
            if last:
                rn, rc = "RC%d" % (gi % 2), RCb[gi % 2]
                yn, yb = "YB%d" % (gi % 2), YBb[gi % 2]
                if window:
                    for r in range(2):
                        act(rc[64:128, r * nq:(r + 1) * nq], PS[ob][64:128, r * nq:(r + 1) * nq], AF.Ln, [P(ob), "ESK"], [rn],
                            bias=ESK[64:128, l, 2 * g + r:2 * g + r + 1])
                    act(rc[64:128, 0:N], rc[64:128, 0:N], AF.Exp, [rn], [rn], scale=-1.0)
                else:
                    recip(rc[64:128, 0:N], PS[ob][64:128, 0:N], [P(ob)], [rn])
                tt("dve", yb[0:64, 0:N], PS[ob][0:64, 0:N], rc[64:128, 0:N], ALU.mult, [P(ob), rn], [yn])
                r0 = row0 + 2 * g * 64
                store(mixT[b][r0:r0 + 128, q0:q0 + nq].rearrange("(r d) q -> d r q", d=64),
                      yb[0:64, 0:N].rearrange("d (r q) -> d r q", r=2), [yn], [("mixT", b)])

        cost = 0.35 if window else 0.55
        for i, st in enumerate(steps):
            issue_S(st)
            if len(pend) > LA:
                finish(pend.pop(0))
                yield cost
        while pend:
            finish(pend.pop(0))
            yield cost

    def cover(seg_base, c0, c1):
        if seg_base == 0:
            return [0]
        r = []
        for (ti, t0, n) in tiles_AE():
            if ti == 0:
                continue
            if t0 < c1 and c0 < t0 + n:
                r.append(ti)
        return r

    def stageE(l, b, ws, need_ctx, after_last_op):
        switch("AE")
        pin, pout = l % 2, (l + 1) % 2
        ft_list = ffn_tiles(need_ctx)

        def geom(fi):
            seg, o0, n_out, seg_len = ft_list[fi]
            N = n_out + 2
            lo = max(o0 - 1, 0)
            hi_ = min(o0 + n_out + 1, seg_len)
            c0 = lo - (o0 - 1)
            c1 = c0 + (hi_ - lo)
            v = b if seg > 0 else NSEQ
            return seg, o0, n_out, seg_len, N, lo, hi_, c0, c1, v

        def op_steps(fi):
            seg, o0, n_out, seg_len, N, lo, hi_, c0, c1, v = geom(fi)
            xn, xt = "X%d" % (fi % 2), Xb[fi % 2]
            an, at = "A%d" % (fi % 2), Ab[fi % 2]
            mn, mt = "MX%d" % (fi % 2), MXb[fi % 2]
            tis = cover(seg, seg + lo, seg + hi_)
            steps = []

            def s_load():
                load(xt[:, :, c0:c1], hTT[pin][b].rearrange("(k p) t -> p k t", p=128)[:, :, seg + lo:seg + hi_],
                     [("hT", pin, b, ti) for ti in tis], [xn])
                load(mt[:, :, c0:c1], mixT[b].rearrange("(k p) t -> p k t", p=128)[:, :, seg + lo:seg + hi_], [("mixT", b)], [mn])
                if c0 > 0:
                    mset("pool", at[:, :, 0:1], 0.0, [an])
                if c1 < N:
                    mset("pool", at[:, :, N - 1:N], 0.0, [an])

            steps.append(s_load)

            def s_unit(u):
                wk, wt = ws.get("wu", ("wout", l, u))
                w3 = wt[:].rearrange("p (k c) -> p k c", k=8)
                for half in range(2):
                    m = 2 * u + half
                    pb = 6 + half
                    for k in range(8):
                        mm(PS[pb][:, c0:c1], w3[:, k, half * 128:(half + 1) * 128], mt[:, k, c0:c1], k == 0, k == 7, [wk, mn], [P(pb)])
                    stt(xt[:, m, c0:c1], PS[pb][:, c0:c1], modsc(l, 2, m, v), xt[:, m, c0:c1], ALU.mult, ALU.add, [P(pb), xn, "MODV"], [xn])

            for u in range(4):
                steps.append(lambda u=u: s_unit(u))
            steps += norm_steps(xn, xt[:, :, c0:c1], an, at[:, :, c0:c1], c1 - c0, A2, 3, l, v, st_bank=2)
            return steps

        nxt = op_steps(0)
        run_steps(nxt)
        for fi in range(len(ft_list)):
            nxt = op_steps(fi + 1) if fi + 1 < len(ft_list) else []
            if not nxt and after_last_op is not None:
                after_last_op()
            seg, o0, n_out, seg_len, N, lo, hi_, c0, c1, v = geom(fi)
            xn, xt = "X%d" % (fi % 2), Xb[fi % 2]
            an, at = "A%d" % (fi % 2), Ab[fi % 2]
            prev_tail = None
            for i in range(NPAIR):
                wk, wt = ws.get("wu", ("wup", l, i))
                w4 = wt[:].rearrange("p (g k m) -> p g k m", g=2, k=8)
                pg = (2 * i) % 6
                pv = (2 * i + 1) % 6
                for k in range(8):
                    mm(PS[pg][:, 0:N], w4[:, 0, k, :], at[:, k, 0:N], k == 0, k == 7, [wk, an], [P(pg)])
                for k in range(8):
                    mm(PS[pv][:, 0:N], w4[:, 1, k, :], at[:, k, 0:N], k == 0, k == 7, [wk, an], [P(pv)])
                if i == 2 and nxt:
                    run_steps(nxt, 1)
                outs = []
                for (pb, ch) in ((pg, i), (pv, NPAIR + i)):
                    fn, ft = Fr.next()
                    fo = (l * 44 + ch) * 3
                    act(ft[:, 0:n_out], PS[pb][:, 0:n_out], AF.Identity, [P(pb), "SM"], [fn], scale=sm("fw", fo), bias=sm("fb", l * 44 + ch))
                    outs.append((fn, ft, pb, fo))
                for (fn, ft, pb, fo) in outs:
                    stt(ft[:, 0:n_out], PS[pb][:, 1:n_out + 1], sm("fw", fo + 1), ft[:, 0:n_out], ALU.mult, ALU.add, [P(pb), fn, "SM"], [fn])
                    stt(ft[:, 0:n_out], PS[pb][:, 2:n_out + 2], sm("fw", fo + 2), ft[:, 0:n_out], ALU.mult, ALU.add, [P(pb), fn, "SM"], [fn])
                if prev_tail is not None:
                    prev_tail()

                def tail(i=i, outs=outs):
                    (gn, gt, _, _), (vn_, vt_, _, _) = outs
                    sn, st_ = Fr.next()
                    act(st_[:, 0:n_out], gt[:, 0:n_out], AF.Silu, [gn], [sn])
                    tt("pool", ACTT[:, i, 0:n_out], st_[:, 0:n_out], vt_[:, 0:n_out], ALU.mult, [sn, vn_], ["ACTT"])

                prev_tail = tail
            prev_tail()
            if nxt:
                run_steps(nxt, 4)
                run_steps(nxt, 3)
            tis_o = cover(seg, seg + o0, seg + o0 + n_out)
            for m in range(8):
                wk, wt = ws.get("wd", ("wdn", l, m))
                pb = 6 + (m % 2)
                for i in range(NPAIR):
                    mm(PS[pb][:, 0:n_out], wt[:, i, :], ACTT[:, i, 0:n_out], i == 0, i == NPAIR - 1, [wk, "ACTT"], [P(pb)])
                stt(xt[:, m, 1:1 + n_out], PS[pb][:, 0:n_out], modsc(l, 5, m, v), xt[:, m, 1:1 + n_out], ALU.mult, ALU.add,
                    [P(pb), xn, "MODV"], [xn])
                run_steps(nxt, 1)
            run_steps(nxt)
            store(hTT[pout][b].rearrange("(k p) t -> p k t", p=128)[:, :, seg + o0:seg + o0 + n_out], xt[:, :, 1:1 + n_out], [xn],
                  [("hT", pout, b, ti) for ti in tis_o])

    def stageF(b):
        switch("AE")
        xi = 0
        for (ti, t0, n) in tiles_AE():
            if ti == 0:
                continue
            xn, xt = "X%d" % (ti % 2), Xb[ti % 2]
            load(xt[:, :, 0:n], hTT[L % 2][b].rearrange("(k p) t -> p k t", p=128)[:, :, t0:t0 + n], [("hT", L % 2, b, ti)], [xn])
            act(XS[:, :, 0:n], xt[:, :, 0:n], AF.Square, [xn], ["XS"])
            for k in range(8):
                mm(PS[0][:, 0:n], ones_s, XS[:, k, 0:n], k == 0, k == 7, ["XS", "CB"], [P(0)])
            rn, rs = "RS0", RSb[0]
            act(rs[:, 0:n], PS[0][:, 0:n], AF.Ln, [P(0)], [rn], bias=EPS)
            act(rs[:, 0:n], rs[:, 0:n], AF.Exp, [rn], [rn], scale=-0.5)
            for k in range(8):
                stt(xt[:, k, 0:n], xt[:, k, 0:n], sm("fng", k), rs[:, 0:n], ALU.mult, ALU.mult, [xn, rn, "SM"], [xn])
            for tb in range(n // 128):
                on, ot = "XIN%d" % (xi % 2), XINb[xi % 2]
                xi += 1
                for half in range(2):
                    pb = 1 + half
                    for kk in range(4):
                        k = half * 4 + kk
                        S.op("pe", lambda e, o=PS[pb][:, kk * 128:(kk + 1) * 128], i=xt[:, k, tb * 128:(tb + 1) * 128]:
                             e.transpose(out=o, in_=i, identity=ident), [xn, "CF"], [P(pb)])
                    cp("act" if half == 0 else "dve", ot[:, half * 512:(half + 1) * 512], PS[pb][:, :], [P(pb)], [on])
                r0 = (t0 - CTX) + tb * 128
                store(out_d[b, r0:r0 + 128, :], ot[:], [on], [("out", b)])

    ws = WStream()
    for b in range(NSEQ):
        stage0(b)
    for l in range(L):
        need_ctx = l < L - 1
        store(WBD[:].rearrange("p a m -> p (a m)"), wbd_d[l], [], ["WBD"])
        for b in range(NSEQ):
            stageA(l, b, ws, first=(l == 0 and b == 0))
        cvt = convert_pieces(l + 1) if l + 1 < L else iter(())
        for b in range(NSEQ):
            switch("BCD")

            def att_chain(l=l, b=b, need_ctx=need_ctx):
                yield from attention(l, b, need_ctx, window=False)
                yield from attention(l, b, need_ctx, window=True)

            gens = [[stageB(l, b), 0.0, True], [att_chain(), ATT_DELAY, True], [cvt, 30.0, True]]
            while any(g[2] for g in gens[:2]):
                live = [g for g in gens if g[2]]
                g = min(live, key=lambda t: t[1])
                try:
                    g[1] += next(g[0]) * (B_SCALE if g is gens[0] else 1.0)
                except StopIteration:
                    g[2] = False
        for _ in cvt:
            pass
        for b in range(NSEQ):
            last = (b == NSEQ - 1) and (l + 1 < L)
            stageE(l, b, ws, need_ctx, (lambda nl=l + 1: load_win(nl, 0)) if last else None)
    for b in range(NSEQ):
        stageF(b)
    S.wait_all("sp")
    S.emit()
    S.close()
    build.stats = (S.n_ins, S.n_wait, hi)
    build.wplan = list(ws.items)
    return nc


def fm(vec):
    v = np.asarray(vec, np.float32)
    lead = v.shape[:-1]
    k = v.shape[-1] // 128
    v = v.reshape(lead + (k, 128))
    return np.moveaxis(v, -1, 0)


def rope_tables(S_len):
    n_freq = 16
    t = np.arange(S_len)
    row = (t // 64).astype(np.float32)
    col = (t % 64).astype(np.float32)
    inv = (np.float32(10000.0) ** (-np.arange(n_freq, dtype=np.float32) / np.float32(n_freq))).astype(np.float32)
    ang = np.concatenate([row[:, None] * inv, col[:, None] * inv], axis=-1).astype(np.float32)
    cos = np.cos(ang).astype(np.float32)
    sin = np.sin(ang).astype(np.float32)
    p = np.arange(128)
    d = p % 64
    j = d % 32
    sign = np.where(d < 32, -1.0, 1.0).astype(np.float32)
    C = cos[:, j].T
    Sg = (sin[:, j] * sign[None, :]).T
    return np.ascontiguousarray(np.stack([C, Sg]).astype(np.float32))


def shared_inputs(inp, S_len, L):
    SO, NS = small_layout(L)
    d = {}
    perm_cols = w_in_perm()
    for l in range(L):
        w = np.asarray(inp["w_in"][l], np.float32)[:, perm_cols]
        w = w.reshape(8, 128, 8, 256)
        d["w_in%d" % l] = np.ascontiguousarray(w.transpose(2, 1, 0, 3).reshape(1024, 2048))
        w = np.asarray(inp["w_out"][l], np.float32).reshape(8, 128, 4, 256)
        d["w_out%d" % l] = np.ascontiguousarray(w.transpose(2, 1, 0, 3).reshape(512, 2048))
        w = np.asarray(inp["w_up"][l], np.float32).reshape(8, 128, 2, NPAIR, 128)
        d["w_up%d" % l] = np.ascontiguousarray(w.transpose(3, 1, 2, 0, 4).reshape(NPAIR * 128, 2048))
        w = np.asarray(inp["w_down"][l], np.float32).reshape(NPAIR, 128, 8, 128)
        d["w_dn%d" % l] = np.ascontiguousarray(w.transpose(2, 1, 0, 3).reshape(8 * 128, NPAIR * 128))
        bd = np.zeros((128, 16, 128), np.float32)
        for gate, key in ((0, "lru_w_a"), (1, "lru_w_x")):
            wg = np.asarray(inp[key][l], np.float32)
            for dd in range(2):
                for c in range(4):
                    a = (gate * 2 + dd) * 4 + c
                    for nl in range(2):
                        bd[nl * 64:(nl + 1) * 64, a, nl * 64:(nl + 1) * 64] = wg[dd, 2 * c + nl]
        d["w_bd%d" % l] = bd.reshape(128, 2048)
    d["w_mod"] = np.ascontiguousarray(np.asarray(inp["w_mod"], np.float32)[:L])
    cf = np.zeros((128, 256), np.float32)
    cf[:, 0:128] = np.eye(128, dtype=np.float32)
    for m in range(128):
        src = m + 32 if (m % 64) < 32 else m - 32
        cf[src, 128 + m] = 1.0
    d["cf32"] = cf
    cb = np.zeros((128, 768), np.float32)
    cb[:, 0:128] = 1.0 / 1024.0
    for h in range(2):
        cb[h * 64:(h + 1) * 64, 128 + h * 64:128 + (h + 1) * 64] = 1.0 / 64.0
    k = np.arange(128)[:, None]
    q = np.arange(128)[None, :]
    ml = (k >= q).astype(np.float32)
    mr = (k <= q).astype(np.float32)
    cb[:, 256:384] = ml
    cb[:, 384:512] = ml
    cb[:, 512:640] = mr
    cb[:, 640:768] = mr
    d["cb16"] = cb.astype(ml_dtypes.bfloat16)
    d["rope"] = rope_tables(S_len)
    return d


def small_pack(inp, L, cvecs):
    SO, NS = small_layout(L)
    sm = np.zeros((128, NS), np.float32)

    def put(name, arr):
        a = np.asarray(arr, np.float32).reshape(128, -1)
        sm[:, SO[name]:SO[name] + a.shape[1]] = a

    put("bmod", fm(np.asarray(inp["b_mod"])[:L]))
    put("n1g", fm(np.asarray(inp["norm1_g"])[:L]))
    put("n2g", fm(np.asarray(inp["norm2_g"])[:L]))
    put("fng", fm(np.asarray(inp["final_norm_g"])))
    cw = fm(np.asarray(inp["lru_conv_w"])[:L])
    put("cw", cw.transpose(0, 1, 3, 2))
    put("cb", fm(np.asarray(inp["lru_conv_b"])[:L]))
    ba = fm(np.asarray(inp["lru_b_a"])[:L])
    bx = fm(np.asarray(inp["lru_b_x"])[:L])
    put("bax", np.stack([ba, bx], axis=2))
    put("lam", fm(np.asarray(inp["lru_lam"])[:L]))
    qg = np.stack([np.asarray(inp["ga_q_norm_g"])[:L], np.asarray(inp["ga_k_norm_g"])[:L]], axis=1)
    qg = np.concatenate([qg, qg], axis=-1)
    put("qg", np.moveaxis(qg, -1, 0))
    sk = np.asarray(inp["wa_sink"], np.float32)[:L]
    put("sink", np.broadcast_to(sk[None], (128, L, 4)))
    fw = fm(np.asarray(inp["ffn_conv_w"])[:L])
    put("fw", fw.transpose(0, 1, 3, 2))
    put("fb", fm(np.asarray(inp["ffn_conv_b"])[:L]))
    cv = np.zeros((128, 8, 4), np.float32)
    for i, c in enumerate(cvecs):
        cv[:, :, i] = fm(c)
    put("cvec", cv)
    return sm


_CACHE = {}


def run(inp, S_len, NSEQ, L, n_cores, trace=False):
    key = (S_len, NSEQ, L)
    if key not in _CACHE:
        build(S_len, NSEQ, L)
        _CACHE[key] = build(S_len, NSEQ, L, wplan=build.wplan)
    nc = _CACHE[key]
    shared = shared_inputs(inp, S_len, L)
    x = np.asarray(inp["x"], np.float32)
    ctx = np.asarray(inp["ctx"], np.float32)
    c = np.asarray(inp["c"], np.float32)
    c_ctx = np.asarray(inp["c_ctx"], np.float32)
    in_maps = []
    for core in range(n_cores):
        b0 = core * NSEQ
        m = dict(shared)
        m["x"] = np.ascontiguousarray(x[b0:b0 + NSEQ, :S_len])
        m["ctx"] = np.ascontiguousarray(ctx[b0:b0 + NSEQ])
        cvecs = [c[b0 + i] for i in range(NSEQ)] + [c_ctx]
        m["small"] = small_pack(inp, L, cvecs)
        in_maps.append(m)
    res = run_bass_kernel_spmd(nc, in_maps, core_ids=list(range(n_cores)), **({"trace": True} if trace else {}))
    out = np.concatenate([np.asarray(r["out"], np.float32) for r in res.results], axis=0)
    return out, res


def kernel(**inputs):
    out, _ = run(inputs, 4096, 2, 4, 8)
    return out
```
